# Optimizing a Trainium2 kernel written in Bass

```python
import math
import jax, jax.numpy as jnp
from jax import lax
import numpy as np

D_MODEL = 1024
BATCH = 2
SEQ = 8192
DEPTH = 2

N_EVEN = (DEPTH + 1) // 2
N_ODD = DEPTH // 2
ALPHA = (2.0 * DEPTH) ** 0.25
BETA = (8.0 * DEPTH) ** -0.25
LN_EPS = 1e-5

POOL_WINDOWS = (2, 4, 8, 16)
POOL_GROUPS = len(POOL_WINDOWS)
POOL_CH = D_MODEL // 8
POOL_WIDTH = POOL_GROUPS * POOL_CH
HEAD_DIM = 64
N_Q_HEADS = D_MODEL // 128
N_KV_HEADS = 2
Q_GROUP = N_Q_HEADS // N_KV_HEADS
ATTN_WIDTH = N_Q_HEADS * HEAD_DIM
KV_WIDTH = N_KV_HEADS * HEAD_DIM
WINDOW = 128
BLOCK = 128
EVEN_IN = POOL_WIDTH + ATTN_WIDTH + 2 * KV_WIDTH
EVEN_MIX = POOL_WIDTH + ATTN_WIDTH

CHUNK = 128
SG_WIDTH = D_MODEL
SG_GROUPS = 8
SG_CH = SG_WIDTH // SG_GROUPS

D_FF = ((8 * D_MODEL // 3 + 127) // 128) * 128
N_EXPERTS = 8
TOP_K = 2
D_FF_EXPERT = 7 * D_MODEL // 2

kernel_name = "hybrid_pool_swa_gmlp_moe_deepnorm_adaln"


def layer_norm(x, g, b):
    xf = x.astype(jnp.float32)
    mu = jnp.mean(xf, axis=-1, keepdims=True)
    var = jnp.mean(jnp.square(xf - mu), axis=-1, keepdims=True)
    return ((xf - mu) * lax.rsqrt(var + LN_EPS) * g + b).astype(x.dtype)


def pool_mixer(p, w_pool, pool_scale):
    B_, S, _ = p.shape
    pf = p.astype(jnp.float32)
    cs = jnp.concatenate([jnp.zeros((B_, 1, POOL_WIDTH), jnp.float32), jnp.cumsum(pf, axis=1)], axis=1)
    t = jnp.arange(S)
    outs = []
    for g, w in enumerate(POOL_WINDOWS):
        r = w // 2
        lo = jnp.maximum(t - r, 0)
        hi = jnp.minimum(t + r + 1, S)
        sl = slice(g * POOL_CH, (g + 1) * POOL_CH)
        csg = cs[..., sl]
        cnt = (hi - lo).astype(jnp.float32)[None, :, None]
        outs.append((csg[:, hi] - csg[:, lo]) / cnt - pf[..., sl])
    pooled = jnp.stack(outs, axis=2).astype(p.dtype)
    y = jnp.einsum('bsgc,gcd->bsgd', pooled, w_pool)
    return y.reshape(B_, S, POOL_WIDTH) * pool_scale


def windowed_gqa(q, k, v, sink):
    B_, S, _, _ = q.shape
    nb = S // BLOCK
    qb = q.reshape(B_, nb, BLOCK, N_KV_HEADS, Q_GROUP, HEAD_DIM)
    pad = ((0, 0), (BLOCK, BLOCK), (0, 0), (0, 0))
    kp = jnp.pad(k, pad).reshape(B_, nb + 2, BLOCK, N_KV_HEADS, HEAD_DIM)
    vp = jnp.pad(v, pad).reshape(B_, nb + 2, BLOCK, N_KV_HEADS, HEAD_DIM)
    kw = jnp.concatenate([kp[:, :-2], kp[:, 1:-1], kp[:, 2:]], axis=2)
    vw = jnp.concatenate([vp[:, :-2], vp[:, 1:-1], vp[:, 2:]], axis=2)
    scores = jnp.einsum('bnqhgd,bnshd->bnhgqs', qb, kw).astype(jnp.float32) * (HEAD_DIM ** -0.5)
    i = jnp.arange(BLOCK)[:, None]
    j = jnp.arange(3 * BLOCK)[None, :]
    dist = jnp.abs(j - BLOCK - i)
    key_pos = jnp.arange(nb)[:, None] * BLOCK - BLOCK + jnp.arange(3 * BLOCK)[None, :]
    valid = (key_pos >= 0) & (key_pos < S)
    mask = (dist <= WINDOW)[None] & valid[:, None, :]
    slopes = 2.0 ** (-8.0 * jnp.arange(1, N_Q_HEADS + 1, dtype=jnp.float32) / N_Q_HEADS)
    alibi = -slopes.reshape(N_KV_HEADS, Q_GROUP)[:, :, None, None] * dist.astype(jnp.float32)
    scores = jnp.where(mask[None, :, None, None], scores + alibi, -1e30)
    sink_b = sink.astype(jnp.float32).reshape(N_KV_HEADS, Q_GROUP)[None, None, :, :, None, None]
    m = jnp.maximum(jnp.max(scores, axis=-1, keepdims=True), sink_b)
    pexp = jnp.exp(scores - m)
    probs = pexp / (jnp.sum(pexp, axis=-1, keepdims=True) + jnp.exp(sink_b - m))
    out = jnp.einsum('bnhgqs,bnshd->bnqhgd', probs.astype(v.dtype), vw)
    return out.reshape(B_, S, ATTN_WIDTH)


def even_mixer(h, w_in, w_pool, pool_scale, sink, w_out):
    B_, S, _ = h.shape
    z = h @ w_in
    p, q, k, v = jnp.split(z, [POOL_WIDTH, POOL_WIDTH + ATTN_WIDTH, POOL_WIDTH + ATTN_WIDTH + KV_WIDTH], axis=-1)
    ya = pool_mixer(p, w_pool, pool_scale)
    yb = windowed_gqa(q.reshape(B_, S, N_Q_HEADS, HEAD_DIM),
                      k.reshape(B_, S, N_KV_HEADS, HEAD_DIM),
                      v.reshape(B_, S, N_KV_HEADS, HEAD_DIM), sink)
    return jnp.concatenate([ya, yb], axis=-1) @ w_out


def spatial_gating_mixer(h, w_in, sg_ln_g, sg_ln_b, w_s, b_s, w_out):
    B_, S, _ = h.shape
    z = jax.nn.gelu(h @ w_in)
    u, v = jnp.split(z, 2, axis=-1)
    v = layer_norm(v, sg_ln_g, sg_ln_b)
    vc = v.reshape(B_, S // CHUNK, CHUNK, SG_GROUPS, SG_CH)
    sv = jnp.einsum('gts,bnsgc->bntgc', w_s, vc) + b_s.T[None, None, :, :, None]
    return (u * sv.reshape(B_, S, SG_WIDTH)) @ w_out


def swiglu(h, w_gate, w_up, w_down):
    return (jax.nn.silu(h @ w_gate) * (h @ w_up)) @ w_down


def moe_swiglu(h, w_router, w_gate, w_up, w_down):
    logits = (h @ w_router).astype(jnp.float32)
    top_v, top_i = lax.top_k(logits, TOP_K)
    gates = jax.nn.softmax(top_v, axis=-1)
    dense_gate = jnp.sum(jax.nn.one_hot(top_i, N_EXPERTS, dtype=jnp.float32) * gates[..., None], axis=-2)
    y = jnp.zeros_like(h)
    for e in range(N_EXPERTS):
        y = y + dense_gate[..., e:e + 1].astype(h.dtype) * swiglu(h, w_gate[e], w_up[e], w_down[e])
    return y


def setup_inputs(seed: int = 0) -> dict:
    key = jax.random.key(seed)
    ks = jax.random.split(key, 32)
    f32 = jnp.float32
    D = D_MODEL

    def nrm(k, shape, scale):
        return jax.random.normal(k, shape, f32) * scale

    x = nrm(ks[0], (BATCH, SEQ, D), 1.0)
    c = nrm(ks[1], (BATCH, D), 1.0)
    ada_w = nrm(ks[2], (DEPTH, D, 6 * D), 0.1 * D ** -0.5)
    ada_b = nrm(ks[3], (DEPTH, 6 * D), 0.01)
    ln_g = 1.0 + nrm(ks[4], (DEPTH, 2, D), 0.02)
    ln_b = nrm(ks[5], (DEPTH, 2, D), 0.01)

    ev_w_in = nrm(ks[6], (N_EVEN, D, EVEN_IN), D ** -0.5)
    col_scale = jnp.concatenate([jnp.full((EVEN_IN - KV_WIDTH,), 1.0, f32), jnp.full((KV_WIDTH,), BETA, f32)])
    ev_w_in = ev_w_in * col_scale
    ev_pool_w = nrm(ks[7], (N_EVEN, POOL_GROUPS, POOL_CH, POOL_CH), POOL_CH ** -0.5)
    ev_pool_scale = 1.0 + nrm(ks[8], (N_EVEN, POOL_WIDTH), 0.02)
    ev_sink = nrm(ks[9], (N_EVEN, N_Q_HEADS), 0.5)
    ev_w_out = nrm(ks[10], (N_EVEN, EVEN_MIX, D), BETA * EVEN_MIX ** -0.5)

    od_w_in = nrm(ks[11], (N_ODD, D, 2 * SG_WIDTH), D ** -0.5)
    od_sg_ln_g = 1.0 + nrm(ks[12], (N_ODD, SG_WIDTH), 0.02)
    od_sg_ln_b = nrm(ks[13], (N_ODD, SG_WIDTH), 0.01)
    od_w_s = nrm(ks[14], (N_ODD, SG_GROUPS, CHUNK, CHUNK), CHUNK ** -0.5)
    od_b_s = 1.0 + nrm(ks[15], (N_ODD, SG_GROUPS, CHUNK), 0.02)
    od_w_out = nrm(ks[16], (N_ODD, SG_WIDTH, D), BETA * SG_WIDTH ** -0.5)

    ffn_w_gate = nrm(ks[17], (N_EVEN, D, D_FF), BETA * D ** -0.5)
    ffn_w_up = nrm(ks[18], (N_EVEN, D, D_FF), BETA * D ** -0.5)
    ffn_w_down = nrm(ks[19], (N_EVEN, D_FF, D), BETA * D_FF ** -0.5)

    moe_w_router = nrm(ks[20], (N_ODD, D, N_EXPERTS), D ** -0.5)
    moe_w_gate = nrm(ks[21], (N_ODD, N_EXPERTS, D, D_FF_EXPERT), BETA * D ** -0.5)
    moe_w_up = nrm(ks[22], (N_ODD, N_EXPERTS, D, D_FF_EXPERT), BETA * D ** -0.5)
    moe_w_down = nrm(ks[23], (N_ODD, N_EXPERTS, D_FF_EXPERT, D), BETA * D_FF_EXPERT ** -0.5)

    return {"x": x, "c": c, "ada_w": ada_w, "ada_b": ada_b, "ln_g": ln_g, "ln_b": ln_b,
            "ev_w_in": ev_w_in, "ev_pool_w": ev_pool_w, "ev_pool_scale": ev_pool_scale,
            "ev_sink": ev_sink, "ev_w_out": ev_w_out,
            "od_w_in": od_w_in, "od_sg_ln_g": od_sg_ln_g, "od_sg_ln_b": od_sg_ln_b,
            "od_w_s": od_w_s, "od_b_s": od_b_s, "od_w_out": od_w_out,
            "ffn_w_gate": ffn_w_gate, "ffn_w_up": ffn_w_up, "ffn_w_down": ffn_w_down,
            "moe_w_router": moe_w_router, "moe_w_gate": moe_w_gate, "moe_w_up": moe_w_up,
            "moe_w_down": moe_w_down}


def reference(x, c, ada_w, ada_b, ln_g, ln_b,
              ev_w_in, ev_pool_w, ev_pool_scale, ev_sink, ev_w_out,
              od_w_in, od_sg_ln_g, od_sg_ln_b, od_w_s, od_b_s, od_w_out,
              ffn_w_gate, ffn_w_up, ffn_w_down,
              moe_w_router, moe_w_gate, moe_w_up, moe_w_down):
    cond = jax.nn.silu(c)
    for l in range(DEPTH):
        mod = (cond @ ada_w[l] + ada_b[l])[:, None, :]
        sh_m, sc_m, g_m, sh_f, sc_f, g_f = jnp.split(mod, 6, axis=-1)
        e = l // 2
        h = x * (1.0 + sc_m) + sh_m
        if l % 2 == 0:
            y = even_mixer(h, ev_w_in[e], ev_pool_w[e], ev_pool_scale[e], ev_sink[e], ev_w_out[e])
        else:
            y = spatial_gating_mixer(h, od_w_in[e], od_sg_ln_g[e], od_sg_ln_b[e], od_w_s[e], od_b_s[e], od_w_out[e])
        x = layer_norm(ALPHA * x + (1.0 + g_m) * y, ln_g[l, 0], ln_b[l, 0])
        h = x * (1.0 + sc_f) + sh_f
        if l % 2 == 0:
            y = swiglu(h, ffn_w_gate[e], ffn_w_up[e], ffn_w_down[e])
        else:
            y = moe_swiglu(h, moe_w_router[e], moe_w_gate[e], moe_w_up[e], moe_w_down[e])
        x = layer_norm(ALPHA * x + (1.0 + g_f) * y, ln_g[l, 1], ln_b[l, 1])
    return x
```

```python
import numpy as np
from contextlib import ExitStack
import concourse.bass as bass
import concourse.mybir as mybir
from concourse.bass_utils import run_bass_kernel_spmd

F32 = mybir.dt.float32
BF16 = mybir.dt.bfloat16
U8 = mybir.dt.uint8
AF = mybir.ActivationFunctionType
ALU = mybir.AluOpType
AX = mybir.AxisListType

NCORES = 8
SEQ = 8192
NT = 2048
NTILE = 16
D = 1024
KC = 8
ALPHA = 2.0 ** 0.5
LN_EPS = 1e-5
D_FF = 2816
D_FFE = 3584
NEXP = 8
import os as _os
NEXP_DECL = int(_os.environ.get("KDBG_NEXP", "8"))
DBG_NOROUTER = bool(_os.environ.get("KDBG_NOROUTER"))
DBG_SKIPCOMPUTE = bool(_os.environ.get("KDBG_SKIPCOMPUTE"))
POOL_WINDOWS = (2, 4, 8, 16)
SLOPES = [2.0 ** (-8.0 * (h + 1) / 8) for h in range(8)]
ARENA = 92 * 1024
SAME_ENG_SYNC = True
GCH = 4


class Sched:
    ENG = ('pe', 'act', 'dve', 'pool', 'sp')

    def __init__(self, nc, stack):
        self.nc = nc
        self.stack = stack
        self.ops = {e: [] for e in self.ENG}
        self.sems = {e: stack.enter_context(nc.semaphore(f"s_{e}")) for e in self.ENG}
        self.cnt = {e: 0 for e in self.ENG}
        self.seen = {e: {} for e in self.ENG}
        self.res = {}
        self.dsem = {}
        self.iv = {}
        self.ov = {}
        self.capture = None

    def replay(self, item):
        if item[0] == 'op':
            self.op(*item[1])
        else:
            q, out, in_, reads, writes, sem, kw = item[1]
            self.dma(q, out, in_, reads, writes, sem, **kw)

    def cap(self, fn, *a):
        prev = self.capture
        self.capture = lst = []
        fn(*a)
        self.capture = prev
        return lst

    def interleave(self, threads, W=2):
        active = []
        nxt = 0
        while nxt < len(threads) or active:
            while len(active) < W and nxt < len(threads) and (not active or active[-1][1] * W >= len(active[-1][0])):
                active.append([threads[nxt], 0])
                nxt += 1
            for t in list(active):
                if t[1] < len(t[0]):
                    self.replay(t[0][t[1]])
                    t[1] += 1
                if t[1] >= len(t[0]):
                    active.remove(t)

    def pipeline(self, stage_fns, n, extra=None, order=None):
        caps = [[self.cap(fn, i) for fn in stage_fns] for i in range(n)]
        ns = len(stage_fns)
        steps = n + ns - 1
        ex = extra or []
        per = (len(ex) + steps - 1) // steps if ex else 0
        for t in range(steps):
            for st_ in (order if order is not None else list(reversed(range(ns)))):
                i = t - st_
                if 0 <= i < n:
                    for item in caps[i][st_]:
                        self.replay(item)
            for item in ex[t * per:(t + 1) * per]:
                self.replay(item)

    def register(self, key, lo, hi):
        assert key not in self.iv, key
        ov = [k for k, (l, h) in self.iv.items() if l < hi and lo < h]
        self.iv[key] = (lo, hi)
        self.ov[key] = ov
        for k in ov:
            self.ov[k].append(key)

    def _deps(self, eng, reads, writes):
        deps = {}

        def add(evs):
            for (k, v) in evs:
                if deps.get(k, 0) < v:
                    deps[k] = v
        for k in reads:
            r = self.res.get(k)
            if r:
                add(r[0])
            for k2 in self.ov.get(k, ()):
                r = self.res.get(k2)
                if r:
                    add(r[0])
        for k in writes:
            r = self.res.get(k)
            if r:
                add(r[0]); add(r[1])
            for k2 in self.ov.get(k, ()):
                r = self.res.get(k2)
                if r:
                    add(r[0]); add(r[1])
        waits = []
        for k, v in deps.items():
            if k == eng and (eng == 'pe' or not SAME_ENG_SYNC):
                continue
            if self.seen[eng].get(k, 0) >= v:
                continue
            self.seen[eng][k] = v
            waits.append((k, v))
        return waits

    def _update(self, ev, reads, writes):
        for k in writes:
            self.res[k] = ([ev], [])
        for k in reads:
            r = self.res.setdefault(k, ([], []))
            lst = [e for e in r[1] if e[0] != ev[0]]
            lst.append(ev)
            self.res[k] = (r[0], lst)

    def op(self, eng, fn, reads=(), writes=(), sig=True):
        if self.capture is not None:
            self.capture.append(('op', (eng, fn, tuple(reads), tuple(writes), sig)))
            return None
        waits = self._deps(eng, reads, writes)
        if sig:
            self.cnt[eng] += 1
            ev = (eng, self.cnt[eng])
        else:
            ev = (eng, self.cnt[eng] + 1)
        self.ops[eng].append((waits, fn, eng if sig else None, 1))
        self._update(ev, reads, writes)
        return ev

    def dma(self, q, out, in_, reads=(), writes=(), sem=None, **kw):
        if self.capture is not None:
            self.capture.append(('dma', (q, out, in_, tuple(reads), tuple(writes), sem, kw)))
            return None
        if sem is None:
            sem = 'd_' + str(writes[0] if writes else reads[0])
        if sem not in self.dsem:
            self.dsem[sem] = [self.stack.enter_context(self.nc.semaphore("q%d" % len(self.dsem))), 0]
        waits = self._deps(q, reads, writes)
        self.dsem[sem][1] += 16
        ev = (sem, self.dsem[sem][1])
        self.ops[q].append((waits, (lambda e, o=out, i=in_, kw=kw: e.dma_start(out=o, in_=i, **kw)), sem, 16))
        self._update(ev, reads, writes)
        return ev

    def wait_sem_final(self, eng, semname):
        self.ops[eng].append(([(semname, self.dsem[semname][1])], None, None, 0))

    def _semof(self, k):
        return self.sems[k] if k in self.sems else self.dsem[k][0]

    def emit(self):
        nc = self.nc
        with nc.Block() as block:
            def run(e, name):
                for (waits, fn, inc, amt) in self.ops[name]:
                    for (k, v) in waits:
                        e.wait_ge(self._semof(k), v)
                    if fn is None:
                        continue
                    ins = fn(e)
                    if inc is not None:
                        ins.then_inc(self._semof(inc), amt)

            @block.tensor
            def _(e):
                run(e, 'pe')

            @block.scalar
            def _(e):
                run(e, 'act')

            @block.vector
            def _(e):
                run(e, 'dve')

            @block.gpsimd
            def _(e):
                run(e, 'pool')

            @block.sync
            def _(e):
                run(e, 'sp')


DTSZ = {F32: 4, BF16: 2, U8: 1}


class Arena:
    def __init__(self, S, tensor, size):
        self.S = S
        self.t = tensor
        self.size = size
        self.off = 0
        self.phase = 0

    def reset(self):
        self.off = 0
        self.phase += 1

    def alloc(self, key, free_shape, dt):
        n = int(np.prod(free_shape)) * DTSZ[dt]
        off = (self.off + 31) // 32 * 32
        assert off + n <= self.size, (key, off, n, self.size)
        self.off = off + n
        ap = self.t[:, off:off + n].bitcast(dt)
        if len(free_shape) > 1:
            names = "abcdef"[:len(free_shape)]
            kw = {nm: int(s) for nm, s in zip(names, free_shape)}
            ap = ap.rearrange("p (%s) -> p %s" % (" ".join(names), " ".join(names)), **kw)
        self.S.register(key, off, off + n)
        return ap


def build_program(stages=(1, 2, 3, 4)):
    stages = tuple(stages)
    nc = bass.Bass("TRN2", target_bir_lowering=False)

    def din(name, shape, dt=F32):
        return nc.dram_tensor(name, list(shape), dt, kind="ExternalInput").ap()

    x_d = din("x", [NT, D])
    xh_d = din("xh", [256, D])
    cT_d = din("cT", [128, KC])
    ident_d = din("ident", [128, 128])
    ada_w_d = din("ada_w", [2, D, 6 * D])
    ada_b_d = din("ada_b", [2, 6 * D])
    ln_g_d = din("ln_g", [2, 2, D])
    ln_b_d = din("ln_b", [2, 2, D])
    ev_w_in_d = din("ev_w_in", [D, 1280])
    ev_pool_w_d = din("ev_pool_w", [4, 128, 128])
    pscale_d = din("pscaleT", [128, 4])
    sink_d = din("ev_sink", [1, 8])
    ev_w_out_d = din("ev_w_out", [D, D])
    band_d = din("band", [128, 7, 4, 128])
    rc_d = din("rc", [1, 2 * 4 * 128])
    dist_d = din("dist", [128, 384])
    dmask_d = din("dmask", [128, 3, 384])
    od_w_in_d = din("od_w_in", [D, 2 * D])
    sg_g_d = din("od_sg_ln_g", [1, D])
    sg_b_d = din("od_sg_ln_b", [1, D])
    wsT_d = din("wsT", [128, 8, 128])
    bsT_d = din("bsT", [128, 8])
    od_w_out_d = din("od_w_out", [D, D])
    ffn_wg_d = din("ffn_w_gate", [D, D_FF])
    ffn_wu_d = din("ffn_w_up", [D, D_FF])
    ffn_wd_d = din("ffn_w_down", [D_FF, D])
    if 4 in stages:
        wr_d = din("moe_w_router", [D, NEXP])
        moe_wg_d = din("moe_w_gate", [NEXP_DECL, D, D_FFE])
        moe_wu_d = din("moe_w_up", [NEXP_DECL, D, D_FFE])
        moe_wd_d = din("moe_w_down", [NEXP_DECL, D_FFE, D])
    out_d = nc.dram_tensor("out", [NT, D], F32, kind="ExternalOutput").ap()
    modscr = nc.dram_tensor("modscr", [2, 6 * D], F32, kind="Internal").ap()

    with ExitStack() as st:
        S = Sched(nc, st)

        def sb(name, shape, dt):
            return st.enter_context(nc.sbuf_tensor(name, shape, dt))

        X = sb("X", [128, NTILE, D], F32)
        hT = sb("hT", [128, KC, NT], BF16)
        ident = sb("ident32", [128, 128], F32)
        identb = sb("identb", [128, 128], BF16)
        tabs = sb("tabs", [128, 2, 4, KC], F32)
        gate = sb("gate", [128, NTILE, NEXP], F32)
        lnbc = sb("lnbc", [128, 2, D], F32)
        gp1 = sb("gp1", [128, D], F32)
        tmpA = sb("tmpA", [128, D], F32)
        small = sb("small", [128, 64], F32)
        epsb = sb("epsb", [128, 1], F32)
        arena_t = sb("arena", [128, ARENA], U8)
        AR = Arena(S, arena_t, ARENA)
        ps = st.enter_context(nc.psum_tensor("ps", [128, 8, 512], F32))

        def PS(b):
            return 'ps%d' % b
        for b_ in range(8):
            S.register('ps%d' % b_, 10 ** 6 + 2048 * b_, 10 ** 6 + 2048 * b_ + 2048)
            for h_ in range(2):
                S.register('ps%dh%d' % (b_, h_), 10 ** 6 + 2048 * b_ + 1024 * h_, 10 ** 6 + 2048 * b_ + 1024 * h_ + 1024)

        def ps2(b):
            return ps[:, b:b + 2, :].rearrange("p a b -> p (a b)")

        def w_view(w2d):
            return w2d.rearrange("(c p) n -> p c n", p=128)

        S.dma('sp', ident[:], ident_d, writes=['ident'])
        S.op('dve', lambda e: e.tensor_copy(out=identb[:], in_=ident[:]), reads=['ident'], writes=['identb'])
        S.op('dve', lambda e: e.memset(epsb[:], LN_EPS), writes=['epsb'])
        for g4 in range(4):
            S.dma('sp', X[:, 4 * g4:4 * g4 + 4, :], x_d[512 * g4:512 * g4 + 512, :].rearrange("(j p) d -> p j d", p=128),
                  writes=[('X', 4 * g4 + i) for i in range(4)], sem='xin%d' % g4)

        AR.reset()
        cT = AR.alloc('p_cT', [KC], F32)
        cond_rep = AR.alloc('p_condrep', [KC, 128], F32)
        mod_bc = AR.alloc('p_modbc', [6 * D], F32)
        adaw_sl = [AR.alloc('p_adaw%d' % i, [KC, 512], F32) for i in range(2)]
        adab_sl = [AR.alloc('p_adab%d' % i, [512], F32) for i in range(2)]
        S.dma('sp', cT, cT_d, writes=['p_cT'])
        S.op('act', lambda e: e.activation(out=cT, in_=cT, func=AF.Silu), reads=['p_cT'], writes=['p_cT'])
        S.op('dve', lambda e: e.tensor_copy(out=cond_rep, in_=cT.unsqueeze(2).to_broadcast([128, KC, 128])),
             reads=['p_cT'], writes=['p_condrep'])
        it = 0
        for l in range(2):
            for blk in range(12):
                sl = it % 2
                bank = it % 4
                it += 1
                S.dma('sp', adaw_sl[sl], w_view(ada_w_d[l])[:, :, 512 * blk:512 * blk + 512], writes=['p_adaw%d' % sl])
                S.dma('sp', adab_sl[sl], ada_b_d[l:l + 1, 512 * blk:512 * blk + 512].partition_broadcast(128),
                      writes=['p_adab%d' % sl])
                for k in range(KC):
                    S.op('pe', lambda e, k=k, sl=sl, bank=bank: e.matmul(ps[:, bank, :], lhsT=cond_rep[:, k, :], rhs=adaw_sl[sl][:, k, :],
                                                                        start=(k == 0), stop=(k == KC - 1)),
                         reads=['p_condrep', 'p_adaw%d' % sl], writes=[PS(bank)], sig=(k == KC - 1))
                S.op('dve', lambda e, sl=sl, bank=bank, blk=blk: e.tensor_tensor(out=mod_bc[:, 512 * blk:512 * blk + 512], in0=ps[:, bank, :],
                                                                              in1=adab_sl[sl], op=ALU.add),
                     reads=[PS(bank), 'p_adab%d' % sl], writes=['p_modbc'])
            for v in (1, 2, 4, 5):
                S.op('dve', lambda e, v=v: e.tensor_scalar_add(out=mod_bc[:, v * D:(v + 1) * D], in0=mod_bc[:, v * D:(v + 1) * D], scalar1=1.0),
                     reads=['p_modbc'], writes=['p_modbc'])
            S.dma('sp', modscr[l:l + 1, :], mod_bc[0:1, :], reads=['p_modbc'], writes=['modscr'], sem='modw')
            for ti, v in enumerate((1, 0, 4, 3)):
                for half in range(2):
                    bank = 4 + (ti * 2 + half) % 4
                    for kk in range(4):
                        k = half * 4 + kk
                        S.op('pe', lambda e, v=v, k=k, kk=kk, bank=bank: e.transpose(ps[:, bank, 128 * kk:128 * kk + 128],
                                                                                   mod_bc[:, v * D + 128 * k: v * D + 128 * k + 128], ident[:]),
                             reads=['p_modbc', 'ident'], writes=[PS(bank)], sig=(kk == 3))
                    S.op('dve', lambda e, l=l, ti=ti, half=half, bank=bank: e.tensor_copy(
                        out=tabs[:, l, ti, 4 * half:4 * half + 4].unsqueeze(2),
                        in_=ps[:, bank, :].rearrange("p (a b) -> p a b", a=4)[:, :, 0:1]),
                        reads=[PS(bank)], writes=['tabs'])

        def load_ln_params(l, which):
            S.dma('sp', lnbc[:, 0, :], ln_g_d[l, which:which + 1, :].partition_broadcast(128), writes=['lnbc'], sem='lnbc')
            S.dma('sp', lnbc[:, 1, :], ln_b_d[l, which:which + 1, :].partition_broadcast(128), writes=['lnbc'], sem='lnbc')

        def load_gp1(l, which):
            blk = 2 if which == 0 else 5
            S.dma('sp', gp1[:], modscr[l:l + 1, blk * D:(blk + 1) * D].partition_broadcast(128), reads=['modscr'], writes=['gp1'])

        ln_rr = [0]

        def emit_ln(j):
            xk = ('X', j)
            o = 16 * (ln_rr[0] % 2)
            tg = 'ln%d' % (ln_rr[0] % 2)
            ln_rr[0] += 1
            st6 = small[:, o:o + 12].rearrange("p (a b) -> p a b", a=2)
            mv = small[:, o + 12:o + 14]
            rstd = small[:, o + 14:o + 15]
            nmr = small[:, o + 15:o + 16]
            for hlf in range(2):
                S.op('dve', lambda e, hlf=hlf: e.bn_stats(out=st6[:, hlf, :], in_=X[:, j, 512 * hlf:512 * hlf + 512]),
                     reads=[xk], writes=[tg + 'st'])
            S.op('dve', lambda e: e.bn_aggr(out=mv, in_=small[:, o:o + 12]), reads=[tg + 'st'], writes=[tg + 'mv'])
            S.op('act', lambda e: e.activation(out=rstd, in_=mv[:, 1:2], func=AF.Ln, bias=epsb[:, 0:1], scale=1.0),
                 reads=[tg + 'mv', 'epsb'], writes=[tg + 'rs'])
            S.op('act', lambda e: e.activation(out=rstd, in_=rstd, func=AF.Exp, scale=-0.5), reads=[tg + 'rs'], writes=[tg + 'rs'])
            S.op('dve', lambda e: e.tensor_scalar(out=nmr, in0=mv[:, 0:1], scalar1=rstd, scalar2=-1.0, op0=ALU.mult, op1=ALU.mult),
                 reads=[tg + 'mv', tg + 'rs'], writes=[tg + 'nm'])
            S.op('act', lambda e: e.activation(out=X[:, j, :], in_=X[:, j, :], func=AF.Identity, bias=nmr, scale=rstd),
                 reads=[xk, tg + 'rs', tg + 'nm'], writes=[xk])
            S.op('dve', lambda e: e.tensor_tensor(out=X[:, j, :], in0=X[:, j, :], in1=lnbc[:, 0, :], op=ALU.mult),
                 reads=[xk, 'lnbc'], writes=[xk])
            S.op('dve', lambda e: e.tensor_tensor(out=X[:, j, :], in0=X[:, j, :], in1=lnbc[:, 1, :], op=ALU.add),
                 reads=[xk, 'lnbc'], writes=[xk])

        tr_rr = [0]

        def emit_hT(src, src_keys, dst_fn, dst_keys, l, ti, banks=(0, 1), router=None):
            for half in range(2):
                bank = banks[half]
                for kk in range(4):
                    k = half * 4 + kk
                    S.op('pe', lambda e, k=k, kk=kk, bank=bank: e.transpose(ps[:, bank, 128 * kk:128 * kk + 128], src[:, 128 * k:128 * k + 128], ident[:]),
                         reads=list(src_keys) + ['ident'], writes=[PS(bank)], sig=(kk == 3))
                for kk in range(4):
                    k = half * 4 + kk
                    if router is None:
                        S.op('act', lambda e, k=k, kk=kk, bank=bank: e.activation(out=dst_fn(k), in_=ps[:, bank, 128 * kk:128 * kk + 128], func=AF.Identity,
                                                                               bias=tabs[:, l, ti + 1, k:k + 1], scale=tabs[:, l, ti, k:k + 1]),
                             reads=[PS(bank), 'tabs'], writes=list(dst_keys))
                    else:
                        h32, h32key = router
                        S.op('act', lambda e, k=k, kk=kk, bank=bank: e.activation(out=h32[:, k, :], in_=ps[:, bank, 128 * kk:128 * kk + 128], func=AF.Identity,
                                                                               bias=tabs[:, l, ti + 1, k:k + 1], scale=tabs[:, l, ti, k:k + 1]),
                             reads=[PS(bank), 'tabs'], writes=[h32key])
                        S.op('dve', lambda e, k=k: e.tensor_copy(out=dst_fn(k), in_=h32[:, k, :]), reads=[h32key], writes=list(dst_keys))

        def hT_own(j):
            return (lambda k: hT[:, k, 128 * j:128 * j + 128])

        def stage1():
            AR.reset()
            winp = AR.alloc('a_winp', [KC, 512], BF16)
            winq = AR.alloc('a_winq', [KC, 512], BF16)
            wink = AR.alloc('a_wink', [KC, 2, 128], BF16)
            winv = AR.alloc('a_winv', [KC, 128], BF16)
            wout = AR.alloc('a_wout', [KC, D], BF16)
            stg = tmpA[:]
            wpool = AR.alloc('a_wpool', [4, 128], BF16)
            band = AR.alloc('a_band', [7, 4, 128], BF16)
            rc = AR.alloc('a_rc', [2, 4, 128], F32)
            dist = AR.alloc('a_dist', [384], F32)
            dmask = AR.alloc('a_dmask', [3, 384], F32)
            sinkb = AR.alloc('a_sink', [8], F32)
            pscale = AR.alloc('a_pscale', [4], F32)
            xh = tmpA[:]
            hTh = AR.alloc('a_hTh', [KC, 128], BF16)
            p_ring = [AR.alloc('a_p%d' % i, [512], BF16) for i in range(4)]
            v_ring = [AR.alloc('a_v%d' % i, [128], BF16) for i in range(4)]
            k_ring = [AR.alloc('a_k%d' % i, [2, 128], BF16) for i in range(4)]
            q_buf = [AR.alloc('a_q%d' % i, [4, 128], BF16) for i in range(4)]
            s_sbP = [[AR.alloc('a_s%d_%d' % (pp, i), [384], F32) for i in range(2)] for pp in range(2)]
            p_bfP = [[AR.alloc('a_pb%d_%d' % (pp, i), [384], BF16) for i in range(2)] for pp in range(2)]
            pTP = [[AR.alloc('a_pT%d_%d' % (pp, i), [3, 128], BF16) for i in range(2)] for pp in range(2)]
            o_sbP = [AR.alloc('a_osb%d' % pp, [512], BF16) for pp in range(2)]
            mixTP = [AR.alloc('a_mixT%d' % pp, [8, 128], BF16) for pp in range(2)]
            poolTP = [AR.alloc('a_poolT%d' % pp, [4, 128], BF16) for pp in range(2)]
            attP = [AR.alloc('a_att%d' % pp, [64], F32) for pp in range(2)]

            wv = w_view(ev_w_in_d)
            S.dma('pool', winp, wv[:, :, 0:512], writes=['a_winp'])
            S.dma('pool', winq, wv[:, :, 512:1024], writes=['a_winq'])
            for g in range(2):
                for dup in range(2):
                    S.dma('pool', wink[:, :, g, 64 * dup:64 * dup + 64], wv[:, :, 1024 + 64 * g:1024 + 64 * g + 64], writes=['a_wink'], sem='a_wink')
            S.dma('pool', winv, wv[:, :, 1152:1280], writes=['a_winv'])
            S.dma('pool', wpool, ev_pool_w_d.rearrange("g c d -> c g d"), writes=['a_wpool'])
            S.dma('pool', band, band_d, writes=['a_band'])
            S.dma('sp', rc, rc_d.partition_broadcast(128).rearrange("p o (a b c) -> p (o a) b c", a=2, b=4), writes=['a_rc'])
            S.dma('sp', dist, dist_d, writes=['a_dist'])
            S.dma('sp', dmask, dmask_d, writes=['a_dmask'])
            S.dma('sp', sinkb, sink_d.partition_broadcast(128).rearrange("p o e -> p (o e)"), writes=['a_sink'])
            S.dma('sp', pscale, pscale_d, writes=['a_pscale'])
            load_gp1(0, 0)
            load_ln_params(0, 0)
            wo_v = w_view(ev_w_out_d)
            for c in range(KC):
                S.dma('sp', stg, wo_v[:, c, :], writes=['tmpA'])
                S.op('dve', lambda e, c=c: e.tensor_tensor(out=wout[:, c, :], in0=stg, in1=gp1[:], op=ALU.mult),
                     reads=['tmpA', 'gp1'], writes=['a_wout'])

            def proj_tile(e_idx):
                slot = e_idx % 4
                own = 1 <= e_idx <= 16
                if own:
                    j = e_idx - 1
                    emit_hT(X[:, j, :], [('X', j)], hT_own(j), [('hT', j)], 0, 0, banks=(0, 0))
                    hsrc = lambda k: hT[:, k, 128 * j:128 * j + 128]
                    hkey = ('hT', j)
                else:
                    r0 = 0 if e_idx == 0 else 128
                    S.dma('sp', xh, xh_d[r0:r0 + 128, :], writes=['tmpA'])
                    emit_hT(xh, ['tmpA'], (lambda k: hTh[:, k, :]), ['a_hTh'], 0, 0, banks=(0, 0))
                    hsrc = lambda k: hTh[:, k, :]
                    hkey = 'a_hTh'
                for k in range(KC):
                    S.op('pe', lambda e, k=k: e.matmul(ps[:, 1, :], lhsT=hsrc(k), rhs=winp[:, k, :], start=(k == 0), stop=(k == KC - 1)),
                         reads=[hkey, 'a_winp'], writes=[PS(1)], sig=(k == KC - 1))
                S.op('act', lambda e: e.activation(out=p_ring[slot], in_=ps[:, 1, :], func=AF.Identity), reads=[PS(1)], writes=['a_p%d' % slot])
                for k in range(KC):
                    S.op('pe', lambda e, k=k: e.matmul(ps[:, 2, 0:128], lhsT=hsrc(k), rhs=winv[:, k, :], start=(k == 0), stop=(k == KC - 1)),
                         reads=[hkey, 'a_winv'], writes=[PS(2)], sig=False)
                for g in range(2):
                    for k in range(KC):
                        S.op('pe', lambda e, k=k, g=g: e.matmul(ps[:, 2, 128 + 128 * g:256 + 128 * g], lhsT=wink[:, k, g, :], rhs=hsrc(k),
                                                               start=(k == 0), stop=(k == KC - 1)),
                             reads=[hkey, 'a_wink'], writes=[PS(2)], sig=(g == 1 and k == KC - 1))
                S.op('dve', lambda e: e.tensor_copy(out=v_ring[slot], in_=ps[:, 2, 0:128]), reads=[PS(2)], writes=['a_v%d' % slot])
                S.op('dve', lambda e: e.tensor_copy(out=k_ring[slot], in_=ps[:, 2, 128:384].rearrange("p (a b) -> p a b", a=2)),
                     reads=[PS(2)], writes=['a_k%d' % slot])
                if own:
                    qs = (e_idx - 1) % 4
                    for c in range(4):
                        for k in range(KC):
                            S.op('pe', lambda e, k=k, c=c: e.matmul(ps[:, 1, 128 * c:128 * c + 128], lhsT=winq[:, k, 128 * c:128 * c + 128], rhs=hsrc(k),
                                                                   start=(k == 0), stop=(k == KC - 1)),
                                 reads=[hkey, 'a_winq'], writes=[PS(1)], sig=(c == 3 and k == KC - 1))
                    S.op('act', lambda e: e.activation(out=q_buf[qs], in_=ps[:, 1, :].rearrange("p (a b) -> p a b", a=4), func=AF.Identity),
                         reads=[PS(1)], writes=['a_q%d' % qs])

            def mix_tile(j, proj_ops=None):
                par = j % 2
                s_sb, p_bf, pT, o_sb, mixT, poolT, att = s_sbP[par], p_bfP[par], pTP[par], o_sbP[par], mixTP[par], poolTP[par], attP[par]
                kmix, kpool, kosb = 'a_mixT%d' % par, 'a_poolT%d' % par, 'a_osb%d' % par
                akeys = [('att', par, h) for h in range(8)]
                var = 0 if j == 0 else (2 if j == NTILE - 1 else 1)
                bsel = {0: (3, 4, 2), 1: (0, 1, 2), 2: (0, 5, 6)}[var]
                for g in range(4):
                    for i3 in range(3):
                        sl = (j + i3) % 4
                        S.op('pe', lambda e, g=g, i3=i3, sl=sl: e.matmul(ps[:, 4, 128 * g:128 * g + 128], lhsT=p_ring[sl][:, 128 * g:128 * g + 128],
                                                                        rhs=band[:, bsel[i3], g, :], start=(i3 == 0), stop=(i3 == 2)),
                             reads=['a_p%d' % sl, 'a_band'], writes=[PS(4)], sig=(g == 3 and i3 == 2))
                if var == 1:
                    for g in range(4):
                        S.op('act', lambda e, g=g: e.activation(out=poolT[:, g, :], in_=ps[:, 4, 128 * g:128 * g + 128], func=AF.Identity,
                                                               scale=1.0 / (POOL_WINDOWS[g] + 1)),
                             reads=[PS(4)], writes=[kpool])
                else:
                    ri = 0 if var == 0 else 1
                    S.op('dve', lambda e: e.tensor_tensor(out=poolT, in0=ps[:, 4, :].rearrange("p (a b) -> p a b", a=4), in1=rc[:, ri, :, :], op=ALU.mult),
                         reads=[PS(4), 'a_rc'], writes=[kpool])
                for g in range(4):
                    S.op('pe', lambda e, g=g: e.matmul(ps[:, 4, 128 * g:128 * g + 128], lhsT=wpool[:, g, :], rhs=poolT[:, g, :], start=True, stop=True),
                         reads=['a_wpool', kpool], writes=[PS(4)], sig=(g == 3))
                for g in range(4):
                    S.op('act', lambda e, g=g: e.activation(out=mixT[:, g, :], in_=ps[:, 4, 128 * g:128 * g + 128], func=AF.Identity, scale=pscale[:, g:g + 1]),
                         reads=[PS(4), 'a_pscale'], writes=[kmix])
                qs = j % 4
                S.op('dve', lambda e: e.memset(att[:, 16:24], 0.0), writes=akeys)

                def hvars(h):
                    c, hp, g = h // 2, h % 2, h // 4
                    sb_i = h % 2
                    return dict(c=c, g=g, sb_i=sb_i, bank=6 + (h % 2), r0=64 * hp, ak=('att', par, h),
                                sk='a_s%d_%d' % (par, sb_i), pk='a_pb%d_%d' % (par, sb_i), tk='a_pT%d_%d' % (par, sb_i),
                                tbank=(3 if h % 2 == 0 else 5))

                def headA(h):
                    v_ = hvars(h)
                    c, g, bank, r0 = v_['c'], v_['g'], v_['bank'], v_['r0']
                    for i3 in range(3):
                        sl = (j + i3) % 4
                        S.op('pe', lambda e, i3=i3, sl=sl: e.matmul(
                            ps[:, bank, 128 * i3:128 * i3 + 128], lhsT=q_buf[qs][r0:r0 + 64, c, :], rhs=k_ring[sl][r0:r0 + 64, g, :], start=True, stop=True),
                            reads=['a_q%d' % qs, 'a_k%d' % sl], writes=[PS(bank)], sig=(i3 == 2))

                def headB(h):
                    v_ = hvars(h)
                    sb_i, bank, ak, sk, pk = v_['sb_i'], v_['bank'], v_['ak'], v_['sk'], v_['pk']
                    S.op('dve', lambda e: e.scalar_tensor_tensor(out=s_sb[sb_i], in0=ps[:, bank, 0:384], scalar=0.125, in1=dmask[:, var, :],
                                                                 op0=ALU.mult, op1=ALU.add),
                         reads=[PS(bank), 'a_dmask'], writes=[sk])
                    S.op('dve', lambda e: e.scalar_tensor_tensor(out=s_sb[sb_i], in0=dist, scalar=-SLOPES[h], in1=s_sb[sb_i],
                                                                 op0=ALU.mult, op1=ALU.add),
                         reads=['a_dist', sk], writes=[sk])
                    S.op('dve', lambda e: e.reduce_max(out=att[:, h:h + 1], in_=s_sb[sb_i], axis=AX.X), reads=[sk], writes=[ak])
                    S.op('dve', lambda e: e.tensor_scalar(out=att[:, 8 + h:9 + h], in0=att[:, h:h + 1], scalar1=sinkb[:, h:h + 1], scalar2=-1.0,
                                                          op0=ALU.max, op1=ALU.mult),
                         reads=[ak, 'a_sink'], writes=[ak])
                    S.op('act', lambda e: e.activation(out=p_bf[sb_i], in_=s_sb[sb_i], func=AF.Exp, bias=att[:, 8 + h:9 + h], scale=1.0,
                                                       accum_out=att[:, 16 + h:17 + h]),
                         reads=[sk, ak], writes=[pk, ak])
                    S.op('act', lambda e: e.activation(out=att[:, 24 + h:25 + h], in_=sinkb[:, h:h + 1], func=AF.Exp, bias=att[:, 8 + h:9 + h], scale=1.0),
                         reads=['a_sink', ak], writes=[ak])

                def headC(h):
                    v_ = hvars(h)
                    sb_i, pk, tk, tbank = v_['sb_i'], v_['pk'], v_['tk'], v_['tbank']
                    psb = ps[:, tbank, :].bitcast(BF16)[:, 0:384]
                    for i3 in range(3):
                        S.op('pe', lambda e, i3=i3: e.transpose(psb[:, 128 * i3:128 * i3 + 128], p_bf[sb_i][:, 128 * i3:128 * i3 + 128], identb[:]),
                             reads=[pk, 'identb'], writes=[PS(tbank)], sig=(i3 == 2))
                    S.op('dve', lambda e: e.tensor_copy(out=pT[sb_i], in_=psb.rearrange("p (a b) -> p a b", a=3)),
                         reads=[PS(tbank)], writes=[tk])

                def headD(h):
                    v_ = hvars(h)
                    sb_i, g, tk = v_['sb_i'], v_['g'], v_['tk']
                    for i3 in range(3):
                        sl = (j + i3) % 4
                        S.op('pe', lambda e, i3=i3, sl=sl: e.matmul(ps[:, 4, 64 * h:64 * h + 64], lhsT=pT[sb_i][:, i3, :],
                                                                   rhs=v_ring[sl][:, 64 * g:64 * g + 64], start=(i3 == 0), stop=(i3 == 2)),
                             reads=[tk, 'a_v%d' % sl], writes=[PS(4)], sig=(i3 == 2))

                S.pipeline([headA, headB, headC, headD], 8, extra=proj_ops)
                S.op('dve', lambda e: e.tensor_tensor(out=att[:, 32:40], in0=att[:, 16:24], in1=att[:, 24:32], op=ALU.add), reads=akeys, writes=akeys)
                S.op('dve', lambda e: e.reciprocal(out=att[:, 32:40], in_=att[:, 32:40]), reads=akeys, writes=akeys)
                S.op('dve', lambda e: e.tensor_tensor(out=o_sb.rearrange("p (a b) -> p a b", a=8), in0=ps[:, 4, :].rearrange("p (a b) -> p a b", a=8),
                                                      in1=att[:, 32:40].unsqueeze(2).to_broadcast([128, 8, 64]), op=ALU.mult),
                     reads=[PS(4)] + akeys, writes=[kosb])
                psb0 = ps[:, 3, :].bitcast(BF16)
                for c in range(4):
                    S.op('pe', lambda e, c=c: e.transpose(psb0[:, 128 * c:128 * c + 128], o_sb[:, 128 * c:128 * c + 128], identb[:]),
                         reads=[kosb, 'identb'], writes=[PS(3)], sig=(c == 3))
                S.op('dve', lambda e: e.tensor_copy(out=mixT[:, 4:8, :], in_=psb0[:, 0:512].rearrange("p (a b) -> p a b", a=4)),
                     reads=[PS(3)], writes=[kmix])
                for n in range(2):
                    for c in range(KC):
                        S.op('pe', lambda e, n=n, c=c: e.matmul(ps[:, 6 + n, :], lhsT=mixT[:, c, :], rhs=wout[:, c, 512 * n:512 * n + 512],
                                                               start=(c == 0), stop=(c == KC - 1)),
                             reads=[kmix, 'a_wout'], writes=[PS(6 + n)], sig=(c == KC - 1))
                S.op('dve', lambda e: e.scalar_tensor_tensor(out=X[:, j, :], in0=X[:, j, :], scalar=ALPHA, in1=ps2(6), op0=ALU.mult, op1=ALU.add),
                     reads=[('X', j), PS(6), PS(7)], writes=[('X', j)])
                emit_ln(j)

            for e_idx in range(3):
                proj_tile(e_idx)
            for j in range(NTILE):
                mix_tile(j, S.cap(proj_tile, j + 3) if j + 3 < 18 else None)

        def make_hT_all(l, ti, router_ctx=None):
            def one(j):
                if router_ctx is None:
                    emit_hT(X[:, j, :], [('X', j)], hT_own(j), [('hT', j)], l, ti, banks=((0, 1) if j % 2 == 0 else (2, 3)))
                else:
                    router_ctx(j)
            S.interleave([S.cap(one, j) for j in range(NTILE)], W=2)

        def stage_ffn(l, moe):
            AR.reset()
            pre = 'f%d_' % l
            wg_sl = [AR.alloc(pre + 'wg%d' % i, [KC, GCH * 128], BF16) for i in range(2)]
            wu_sl = [AR.alloc(pre + 'wu%d' % i, [KC, GCH * 128], BF16) for i in range(2)]
            wd_sl = [AR.alloc(pre + 'wd%d' % i, [GCH, D], BF16) for i in range(2)]
            stg = AR.alloc(pre + 'stg', [GCH, D], F32)
            aT = [AR.alloc(pre + 'aT%d' % i, [GCH, 512], BF16) for i in range(2)]
            sg = [AR.alloc(pre + 'sg%d' % i, [512], F32) for i in range(2)]
            load_gp1(l, 1)
            load_ln_params(l, 1)
            if moe:
                h32b = AR.alloc(pre + 'h32b', [KC, 128], F32)
                h32P = [(tmpA[:].rearrange("p (a b) -> p a b", a=KC), 'tmpA'), (h32b, pre + 'h32b')]
                wr = AR.alloc(pre + 'wr', [KC, NEXP], F32)
                rtP = [AR.alloc(pre + 'rt%d' % i, [64], F32) for i in range(2)]
                S.dma('sp', wr, w_view(wr_d), writes=[pre + 'wr'])

                def router_ctx(j):
                    par = j % 2
                    h32, hkey32 = h32P[par]
                    rt = rtP[par]
                    rb = 6 + par
                    emit_hT(X[:, j, :], [('X', j)], hT_own(j), [('hT', j)], l, 2, banks=((0, 1) if par == 0 else (2, 3)), router=(h32, hkey32))
                    for k in range(KC):
                        S.op('pe', lambda e, k=k: e.matmul(ps[:, rb, 0:NEXP], lhsT=h32[:, k, :], rhs=wr[:, k, :], start=(k == 0), stop=(k == KC - 1)),
                             reads=[hkey32, pre + 'wr'], writes=[PS(rb)], sig=(k == KC - 1))
                    lg, eq, l2, ex = rt[:, 0:8], rt[:, 8:16], rt[:, 16:24], rt[:, 24:32]
                    m1, m2, nm1, den = rt[:, 32:33], rt[:, 33:34], rt[:, 34:35], rt[:, 35:36]
                    rk = pre + 'rt%d' % par
                    S.op('dve', lambda e: e.tensor_copy(out=lg, in_=ps[:, rb, 0:NEXP]), reads=[PS(rb)], writes=[rk])
                    S.op('dve', lambda e: e.reduce_max(out=m1, in_=lg, axis=AX.X), reads=[rk], writes=[rk])
                    S.op('dve', lambda e: e.tensor_scalar(out=eq, in0=lg, scalar1=m1, scalar2=None, op0=ALU.is_equal), reads=[rk], writes=[rk])
                    S.op('dve', lambda e: e.scalar_tensor_tensor(out=l2, in0=eq, scalar=-1e30, in1=lg, op0=ALU.mult, op1=ALU.add), reads=[rk], writes=[rk])
                    S.op('dve', lambda e: e.reduce_max(out=m2, in_=l2, axis=AX.X), reads=[rk], writes=[rk])
                    S.op('dve', lambda e: e.tensor_scalar(out=eq, in0=lg, scalar1=m2, scalar2=None, op0=ALU.is_ge), reads=[rk], writes=[rk])
                    S.op('dve', lambda e: e.tensor_scalar_mul(out=nm1, in0=m1, scalar1=-1.0), reads=[rk], writes=[rk])
                    S.op('act', lambda e: e.activation(out=ex, in_=lg, func=AF.Exp, bias=nm1, scale=1.0), reads=[rk], writes=[rk])
                    S.op('dve', lambda e: e.tensor_tensor(out=ex, in0=ex, in1=eq, op=ALU.mult), reads=[rk], writes=[rk])
                    S.op('dve', lambda e: e.reduce_sum(out=den, in_=ex, axis=AX.X), reads=[rk], writes=[rk])
                    S.op('dve', lambda e: e.reciprocal(out=den, in_=den), reads=[rk], writes=[rk])
                    S.op('dve', lambda e: e.tensor_scalar(out=gate[:, j, :], in0=ex, scalar1=den, scalar2=None, op0=ALU.mult), reads=[rk], writes=[('gate', j)])
                if DBG_NOROUTER:
                    make_hT_all(l, 2)
                    for j in range(NTILE):
                        S.op('dve', lambda e, j=j: e.memset(gate[:, j, :], 0.125), writes=[('gate', j)])
                else:
                    make_hT_all(l, 2, router_ctx)
                experts = [(moe_wg_d[e_], moe_wu_d[e_], moe_wd_d[e_]) for e_ in range(NEXP_DECL)]
                ff = D_FFE
            else:
                make_hT_all(l, 2)
                experts = [(ffn_wg_d, ffn_wu_d, ffn_wd_d)]
                ff = D_FF
            if moe:
                for j in range(NTILE):
                    S.op('act', lambda e, j=j: e.activation(out=X[:, j, :], in_=X[:, j, :], func=AF.Identity, scale=ALPHA),
                         reads=[('X', j)], writes=[('X', j)])

            nch = ff // 128
            groups = []
            for ei in range(len(experts)):
                c0 = 0
                while c0 < nch:
                    gc = min(GCH, nch - c0)
                    groups.append((ei, c0, gc))
                    c0 += gc

            import os
            if os.environ.get("KDBG_NG"):
                groups = groups[:int(os.environ["KDBG_NG"])]

            def load_group(gi):
                ei, c0, gc = groups[gi]
                sl = gi % 2
                wg_d, wu_d, wd_d = experts[ei]
                f0, f1 = 128 * c0, 128 * (c0 + gc)
                S.dma('pool', wg_sl[sl][:, :, 0:128 * gc], w_view(wg_d)[:, :, f0:f1], writes=[pre + 'wg%d' % sl])
                S.dma('pool', wu_sl[sl][:, :, 0:128 * gc], w_view(wu_d)[:, :, f0:f1], writes=[pre + 'wu%d' % sl])
                S.dma('sp', stg[:, 0:gc, :], wd_d[f0:f1, :].rearrange("(c p) n -> p c n", p=128), writes=[pre + 'stg'])
                for c in range(gc):
                    S.op('dve', lambda e, c=c, sl=sl: e.tensor_tensor(out=wd_sl[sl][:, c, :], in0=stg[:, c, :], in1=gp1[:], op=ALU.mult),
                         reads=[pre + 'stg', 'gp1'], writes=[pre + 'wd%d' % sl])

            def upgate(sl, gc, tb, a_sl):
                hkeys = [('hT', 4 * tb + i) for i in range(4)]
                for c in range(gc):
                    gb = c % 2
                    for k in range(KC):
                        S.op('pe', lambda e, k=k, c=c, gb=gb: e.matmul(ps[:, gb, :], lhsT=wg_sl[sl][:, k, 128 * c:128 * c + 128], rhs=hT[:, k, 512 * tb:512 * tb + 512],
                                                                     start=(k == 0), stop=(k == KC - 1)),
                             reads=hkeys + [pre + 'wg%d' % sl], writes=[PS(gb)], sig=(k == KC - 1))
                    for k in range(KC):
                        S.op('pe', lambda e, k=k, c=c, gb=gb: e.matmul(ps[:, 2 + gb, :], lhsT=wu_sl[sl][:, k, 128 * c:128 * c + 128], rhs=hT[:, k, 512 * tb:512 * tb + 512],
                                                                     start=(k == 0), stop=(k == KC - 1)),
                             reads=hkeys + [pre + 'wu%d' % sl], writes=[PS(2 + gb)], sig=(k == KC - 1))
                    S.op('act', lambda e, gb=gb: e.activation(out=sg[gb], in_=ps[:, gb, :], func=AF.Silu), reads=[PS(gb)], writes=[pre + 'sg%d' % gb])
                    S.op('dve', lambda e, c=c, gb=gb: e.tensor_tensor(out=aT[a_sl][:, c, :], in0=sg[gb], in1=ps[:, 2 + gb, :], op=ALU.mult),
                         reads=[pre + 'sg%d' % gb, PS(2 + gb)], writes=[pre + 'aT%d' % a_sl])

            def down(sl, gc, ei, tb, a_sl, last_group, first_group):
                for t4 in range(4):
                    j = 4 * tb + t4
                    yb = 4 + 2 * (t4 % 2)
                    for n in range(2):
                        for c in range(gc):
                            S.op('pe', lambda e, n=n, c=c, t4=t4, yb=yb: e.matmul(ps[:, yb + n, :], lhsT=aT[a_sl][:, c, 128 * t4:128 * t4 + 128],
                                                                                rhs=wd_sl[sl][:, c, 512 * n:512 * n + 512],
                                                                                start=(c == 0), stop=(c == gc - 1)),
                                 reads=[pre + 'aT%d' % a_sl, pre + 'wd%d' % sl], writes=[PS(yb + n)], sig=(c == gc - 1))
                    if moe:
                        S.op('dve', lambda e, j=j, yb=yb: e.scalar_tensor_tensor(out=X[:, j, :], in0=ps2(yb), scalar=gate[:, j, ei:ei + 1], in1=X[:, j, :],
                                                                               op0=ALU.mult, op1=ALU.add),
                             reads=[PS(yb), PS(yb + 1), ('X', j), ('gate', j)], writes=[('X', j)])
                    elif first_group:
                        S.op('dve', lambda e, j=j, yb=yb: e.scalar_tensor_tensor(out=X[:, j, :], in0=X[:, j, :], scalar=ALPHA, in1=ps2(yb),
                                                                               op0=ALU.mult, op1=ALU.add),
                             reads=[PS(yb), PS(yb + 1), ('X', j)], writes=[('X', j)])
                    else:
                        S.op('dve', lambda e, j=j, yb=yb: e.tensor_tensor(out=X[:, j, :], in0=ps2(yb), in1=X[:, j, :], op=ALU.add),
                             reads=[PS(yb), PS(yb + 1), ('X', j)], writes=[('X', j)])
                    if last_group:
                        emit_ln(j)

            if DBG_SKIPCOMPUTE:
                return
            G = len(groups)
            items = [(gi, tb) for gi in range(G) for tb in range(4)]
            load_group(0)
            if G > 1:
                load_group(1)
            upgate(0, groups[0][2], 0, 0)
            for n, (gi, tb) in enumerate(items):
                if n + 1 < len(items):
                    gi2, tb2 = items[n + 1]
                    upgate(gi2 % 2, groups[gi2][2], tb2, (n + 1) % 2)
                down(gi % 2, groups[gi][2], groups[gi][0], tb, n % 2, gi == G - 1, gi == 0)
                if tb == 3 and gi + 2 < G:
                    load_group(gi + 2)

        def stage3():
            AR.reset()
            winu = AR.alloc('b_winu', [KC, D], BF16)
            winv = AR.alloc('b_winv', [KC, D], BF16)
            wout = AR.alloc('b_wout', [KC, D], BF16)
            stg = AR.alloc('b_stg', [D], F32)
            wsT = AR.alloc('b_wsT', [8, 128], BF16)
            bsT = AR.alloc('b_bsT', [8], F32)
            sgbc = AR.alloc('b_sgbc', [2, D], F32)
            u32 = [AR.alloc('b_u32%d' % i, [D], F32) for i in range(2)]
            v32 = [AR.alloc('b_v32%d' % i, [D], F32) for i in range(2)]
            vln = [AR.alloc('b_vln%d' % i, [D], BF16) for i in range(2)]
            gat = [AR.alloc('b_gat%d' % i, [D], BF16) for i in range(2)]
            gatT = [AR.alloc('b_gatT%d' % i, [8, 128], BF16) for i in range(2)]
            sm = [AR.alloc('b_sm%d' % i, [16], F32) for i in range(2)]
            wv = w_view(od_w_in_d)
            S.dma('pool', winu, wv[:, :, 0:D], writes=['b_winu'])
            S.dma('pool', winv, wv[:, :, D:2 * D], writes=['b_winv'])
            S.dma('pool', wsT, wsT_d, writes=['b_wsT'])
            S.dma('sp', bsT, bsT_d, writes=['b_bsT'])
            S.dma('sp', sgbc[:, 0, :], sg_g_d.partition_broadcast(128).rearrange("p o d -> p (o d)"), writes=['b_sgbc'], sem='b_sgbc')
            S.dma('sp', sgbc[:, 1, :], sg_b_d.partition_broadcast(128).rearrange("p o d -> p (o d)"), writes=['b_sgbc'], sem='b_sgbc')
            load_gp1(1, 0)
            load_ln_params(1, 0)
            wo_v = w_view(od_w_out_d)
            for c in range(KC):
                S.dma('sp', stg, wo_v[:, c, :], writes=['b_stg'])
                S.op('dve', lambda e, c=c: e.tensor_tensor(out=wout[:, c, :], in0=stg, in1=gp1[:], op=ALU.mult),
                     reads=['b_stg', 'gp1'], writes=['b_wout'])
            make_hT_all(1, 0)

            def names3(j):
                b = j % 2
                return dict(b=b, sw=4 * b + 2, u_=u32[b], v_=v32[b], vl_=vln[b], ga_=gat[b], gT_=gatT[b], sm_=sm[b],
                            ku='b_u32%d' % b, kv='b_v32%d' % b, kvl='b_vln%d' % b, kga='b_gat%d' % b, kgT='b_gatT%d' % b, ksm='b_sm%d' % b)

            def st3A(j):
                n_ = names3(j)
                b = n_['b']
                hk = ('hT', j)
                for (w_, dst, dkey, wkey, b0) in ((winu, n_['u_'], n_['ku'], 'b_winu', 4 * b), (winv, n_['v_'], n_['kv'], 'b_winv', 4 * b)):
                    for n in range(2):
                        for k in range(KC):
                            S.op('pe', lambda e, n=n, k=k, w_=w_, b0=b0: e.matmul(ps[:, b0 + n, :], lhsT=hT[:, k, 128 * j:128 * j + 128], rhs=w_[:, k, 512 * n:512 * n + 512],
                                                                                start=(k == 0), stop=(k == KC - 1)),
                                 reads=[hk, wkey], writes=[PS(b0 + n)], sig=(k == KC - 1))
                    S.op('act', lambda e, dst=dst, b0=b0: e.activation(out=dst, in_=ps2(b0), func=AF.Gelu), reads=[PS(b0), PS(b0 + 1)], writes=[dkey])

            def st3B(j):
                n_ = names3(j)
                v_, vl_, sm_, kv, kvl, ksm = n_['v_'], n_['vl_'], n_['sm_'], n_['kv'], n_['kvl'], n_['ksm']
                st6 = sm_[:, 0:12].rearrange("p (a b) -> p a b", a=2)
                mv, rstd, nmr = sm_[:, 12:14], sm_[:, 14:15], sm_[:, 15:16]
                for hlf in range(2):
                    S.op('dve', lambda e, hlf=hlf: e.bn_stats(out=st6[:, hlf, :], in_=v_[:, 512 * hlf:512 * hlf + 512]), reads=[kv], writes=[ksm])
                S.op('dve', lambda e: e.bn_aggr(out=mv, in_=sm_[:, 0:12]), reads=[ksm], writes=[ksm])
                S.op('act', lambda e: e.activation(out=rstd, in_=mv[:, 1:2], func=AF.Ln, bias=epsb[:, 0:1], scale=1.0), reads=[ksm, 'epsb'], writes=[ksm])
                S.op('act', lambda e: e.activation(out=rstd, in_=rstd, func=AF.Exp, scale=-0.5), reads=[ksm], writes=[ksm])
                S.op('dve', lambda e: e.tensor_scalar(out=nmr, in0=mv[:, 0:1], scalar1=rstd, scalar2=-1.0, op0=ALU.mult, op1=ALU.mult), reads=[ksm], writes=[ksm])
                S.op('act', lambda e: e.activation(out=v_, in_=v_, func=AF.Identity, bias=nmr, scale=rstd), reads=[kv, ksm], writes=[kv])
                S.op('dve', lambda e: e.tensor_tensor(out=v_, in0=v_, in1=sgbc[:, 0, :], op=ALU.mult), reads=[kv, 'b_sgbc'], writes=[kv])
                S.op('dve', lambda e: e.tensor_tensor(out=vl_, in0=v_, in1=sgbc[:, 1, :], op=ALU.add), reads=[kv, 'b_sgbc'], writes=[kvl])

            def st3C(j):
                n_ = names3(j)
                sw, u_, v_, vl_, ga_, ku, kv, kvl, kga = n_['sw'], n_['u_'], n_['v_'], n_['vl_'], n_['ga_'], n_['ku'], n_['kv'], n_['kvl'], n_['kga']
                for g in range(8):
                    S.op('pe', lambda e, g=g: e.matmul(ps[:, sw + g // 4, 128 * (g % 4):128 * (g % 4) + 128], lhsT=wsT[:, g, :], rhs=vl_[:, 128 * g:128 * g + 128],
                                                       start=True, stop=True),
                         reads=['b_wsT', kvl], writes=[PS(sw + g // 4)], sig=(g % 4 == 3))
                S.op('dve', lambda e: e.tensor_tensor(out=v_.rearrange("p (a b) -> p a b", a=8), in0=ps2(sw).rearrange("p (a b) -> p a b", a=8),
                                                      in1=bsT[:, 0:8].unsqueeze(2).to_broadcast([128, 8, 128]), op=ALU.add),
                     reads=[PS(sw), PS(sw + 1), 'b_bsT'], writes=[kv])
                S.op('dve', lambda e: e.tensor_tensor(out=ga_, in0=v_, in1=u_, op=ALU.mult), reads=[kv, ku], writes=[kga])

            def st3D(j):
                n_ = names3(j)
                b, ga_, gT_, kga, kgT = n_['b'], n_['ga_'], n_['gT_'], n_['kga'], n_['kgT']
                pbank = 4 * b
                psb = ps[:, pbank, :].bitcast(BF16)
                for g in range(8):
                    S.op('pe', lambda e, g=g: e.transpose(psb[:, 128 * g:128 * g + 128], ga_[:, 128 * g:128 * g + 128], identb[:]),
                         reads=[kga, 'identb'], writes=[PS(pbank)], sig=(g == 7))
                S.op('act', lambda e: e.activation(out=gT_, in_=psb.rearrange("p (a b) -> p a b", a=8), func=AF.Identity), reads=[PS(pbank)], writes=[kgT])

            def st3E(j):
                n_ = names3(j)
                sw, gT_, kgT = n_['sw'], n_['gT_'], n_['kgT']
                for n in range(2):
                    for c in range(KC):
                        S.op('pe', lambda e, n=n, c=c: e.matmul(ps[:, sw + n, :], lhsT=gT_[:, c, :], rhs=wout[:, c, 512 * n:512 * n + 512],
                                                               start=(c == 0), stop=(c == KC - 1)),
                             reads=[kgT, 'b_wout'], writes=[PS(sw + n)], sig=(c == KC - 1))
                S.op('dve', lambda e: e.scalar_tensor_tensor(out=X[:, j, :], in0=X[:, j, :], scalar=ALPHA, in1=ps2(sw), op0=ALU.mult, op1=ALU.add),
                     reads=[('X', j), PS(sw), PS(sw + 1)], writes=[('X', j)])
                emit_ln(j)

            S.pipeline([st3A, st3B, st3C, st3D, st3E], NTILE, order=[3, 4, 2, 0, 1])

        for s in stages:
            if s == 1:
                stage1()
            elif s == 2:
                stage_ffn(0, False)
            elif s == 3:
                stage3()
            elif s == 4:
                stage_ffn(1, True)

        for g4 in range(4):
            S.dma('sp', out_d[512 * g4:512 * g4 + 512, :].rearrange("(j p) d -> p j d", p=128), X[:, 4 * g4:4 * g4 + 4, :],
                  reads=[('X', 4 * g4 + i) for i in range(4)], sem='out')
        S.wait_sem_final('sp', 'out')
        S.emit()
    return nc


def _const_tables(q):
    t0 = q * NT
    band = np.zeros((128, 7, 4, 128), np.float32)
    rc = np.zeros((2, 4, 128), np.float32)
    s_loc = np.arange(128)[:, None]
    t_loc = np.arange(128)[None, :]
    for g, w in enumerate(POOL_WINDOWS):
        r = w // 2

        def mats(tile_base):
            tg = tile_base + t_loc
            cnt = (np.minimum(tg + r + 1, SEQ) - np.maximum(tg - r, 0)).astype(np.float32)
            outs = []
            for off in (-128, 0, 128):
                sg_ = tile_base + off + s_loc
                m = ((np.abs(sg_ - tg) <= r) & (sg_ >= 0) & (sg_ < SEQ)).astype(np.float32)
                if off == 0:
                    m = m - np.eye(128, dtype=np.float32) * cnt
                outs.append(m)
            return outs, cnt
        mid, _ = mats(t0 + 128 * 5)
        first, cf = mats(t0)
        last, cl = mats(t0 + NT - 128)
        band[:, 0, g], band[:, 1, g], band[:, 2, g] = mid
        band[:, 3, g], band[:, 4, g] = first[0], first[1]
        band[:, 5, g], band[:, 6, g] = last[1], last[2]
        rc[0, g] = 1.0 / cf[0]
        rc[1, g] = 1.0 / cl[0]
    i = np.arange(128)[:, None]
    jj = np.arange(384)[None, :]
    dist = np.abs(jj - 128 - i).astype(np.float32)
    dmask = np.zeros((128, 3, 384), np.float32)
    for var, base in enumerate((t0, t0 + 128 * 5, t0 + NT - 128)):
        kp = base - 128 + jj
        ok = (dist <= 128) & (kp >= 0) & (kp < SEQ)
        dmask[:, var, :] = np.where(ok, 0.0, -1e30)
    return band, rc.reshape(1, -1), dist, dmask


_PROG_CACHE = {}


def _get_prog(stages):
    if stages not in _PROG_CACHE:
        _PROG_CACHE[stages] = build_program(stages)
    return _PROG_CACHE[stages]


def _in_maps(inputs, x_full, stages=(1, 2, 3, 4)):
    f = lambda a: np.ascontiguousarray(np.asarray(a, dtype=np.float32))
    shared = dict(
        ident=np.eye(128, dtype=np.float32),
        ada_w=f(inputs["ada_w"]), ada_b=f(inputs["ada_b"]), ln_g=f(inputs["ln_g"]), ln_b=f(inputs["ln_b"]),
        ev_w_in=f(inputs["ev_w_in"][0]), ev_pool_w=f(inputs["ev_pool_w"][0]),
        pscaleT=f(np.asarray(inputs["ev_pool_scale"][0]).reshape(4, 128).T),
        ev_sink=f(np.asarray(inputs["ev_sink"][0]).reshape(1, 8)), ev_w_out=f(inputs["ev_w_out"][0]),
        od_w_in=f(inputs["od_w_in"][0]), od_sg_ln_g=f(np.asarray(inputs["od_sg_ln_g"][0]).reshape(1, D)),
        od_sg_ln_b=f(np.asarray(inputs["od_sg_ln_b"][0]).reshape(1, D)),
        wsT=f(np.asarray(inputs["od_w_s"][0]).transpose(2, 0, 1)),
        bsT=f(np.asarray(inputs["od_b_s"][0]).T),
        od_w_out=f(inputs["od_w_out"][0]),
        ffn_w_gate=f(inputs["ffn_w_gate"][0]), ffn_w_up=f(inputs["ffn_w_up"][0]), ffn_w_down=f(inputs["ffn_w_down"][0]),
    )
    if 4 in stages:
        shared.update(moe_w_router=f(inputs["moe_w_router"][0]), moe_w_gate=f(inputs["moe_w_gate"][0][:NEXP_DECL]),
                      moe_w_up=f(inputs["moe_w_up"][0][:NEXP_DECL]), moe_w_down=f(inputs["moe_w_down"][0][:NEXP_DECL]))
    c = np.asarray(inputs["c"], dtype=np.float32)
    maps = []
    for core in range(NCORES):
        b, q = core // 4, core % 4
        t0 = q * NT
        xh = np.zeros((256, D), np.float32)
        if q > 0:
            xh[0:128] = x_full[b, t0 - 128:t0]
        if q < 3:
            xh[128:256] = x_full[b, t0 + NT:t0 + NT + 128]
        band, rc, dist, dmask = _const_tables(q)
        m = dict(shared)
        m.update(x=np.ascontiguousarray(x_full[b, t0:t0 + NT]), xh=xh,
                 cT=np.ascontiguousarray(c[b].reshape(KC, 128).T),
                 band=band, rc=rc, dist=dist, dmask=dmask)
        maps.append(m)
    return maps


def run_stages(inputs, x_full, stages):
    nc = _get_prog(tuple(stages))
    maps = _in_maps(inputs, x_full, tuple(stages))
    res = run_bass_kernel_spmd(nc, maps, core_ids=list(range(NCORES)))
    out = np.empty((2, SEQ, D), np.float32)
    for core in range(NCORES):
        b, q = core // 4, core % 4
        out[b, q * NT:(q + 1) * NT] = res.results[core]["out"]
    return out


def kernel(**inputs):
    x = np.asarray(inputs["x"], dtype=np.float32)
    return run_stages(inputs, x, (1, 2, 3, 4))
```

```python
import numpy as np
from contextlib import ExitStack
import concourse.bass as bass
import concourse.mybir as mybir
from concourse.bass_utils import run_bass_kernel_spmd

F32 = mybir.dt.float32
BF16 = mybir.dt.bfloat16
U8 = mybir.dt.uint8
AF = mybir.ActivationFunctionType
ALU = mybir.AluOpType
AX = mybir.AxisListType

NCORES = 8
SEQ = 8192
NT = 2048
NTILE = 16
D = 1024
KC = 8
ALPHA = 2.0 ** 0.5
LN_EPS = 1e-5
D_FF = 2816
D_FFE = 3584
NEXP = 8
import os as _os
NEXP_DECL = int(_os.environ.get("KDBG_NEXP", "8"))
DBG_NOROUTER = bool(_os.environ.get("KDBG_NOROUTER"))
DBG_SKIPCOMPUTE = bool(_os.environ.get("KDBG_SKIPCOMPUTE"))
POOL_WINDOWS = (2, 4, 8, 16)
SLOPES = [2.0 ** (-8.0 * (h + 1) / 8) for h in range(8)]
ARENA = 92 * 1024
SAME_ENG_SYNC = True
GCH = 4


class Sched:
    ENG = ('pe', 'act', 'dve', 'pool', 'sp')

    def __init__(self, nc, stack):
        self.nc = nc
        self.stack = stack
        self.ops = {e: [] for e in self.ENG}
        self.sems = {e: stack.enter_context(nc.semaphore(f"s_{e}")) for e in self.ENG}
        self.cnt = {e: 0 for e in self.ENG}
        self.seen = {e: {} for e in self.ENG}
        self.res = {}
        self.dsem = {}
        self.iv = {}
        self.ov = {}
        self.capture = None

    def replay(self, item):
        if item[0] == 'op':
            self.op(*item[1])
        else:
            q, out, in_, reads, writes, sem, kw = item[1]
            self.dma(q, out, in_, reads, writes, sem, **kw)

    def cap(self, fn, *a):
        prev = self.capture
        self.capture = lst = []
        fn(*a)
        self.capture = prev
        return lst

    def interleave(self, threads, W=2):
        active = []
        nxt = 0
        while nxt < len(threads) or active:
            while len(active) < W and nxt < len(threads) and (not active or active[-1][1] * W >= len(active[-1][0])):
                active.append([threads[nxt], 0])
                nxt += 1
            for t in list(active):
                if t[1] < len(t[0]):
                    self.replay(t[0][t[1]])
                    t[1] += 1
                if t[1] >= len(t[0]):
                    active.remove(t)

    def pipeline(self, stage_fns, n, extra=None, order=None):
        caps = [[self.cap(fn, i) for fn in stage_fns] for i in range(n)]
        ns = len(stage_fns)
        steps = n + ns - 1
        ex = extra or []
        per = (len(ex) + steps - 1) // steps if ex else 0
        for t in range(steps):
            for st_ in (order if order is not None else list(reversed(range(ns)))):
                i = t - st_
                if 0 <= i < n:
                    for item in caps[i][st_]:
                        self.replay(item)
            for item in ex[t * per:(t + 1) * per]:
                self.replay(item)

    def register(self, key, lo, hi):
        assert key not in self.iv, key
        ov = [k for k, (l, h) in self.iv.items() if l < hi and lo < h]
        self.iv[key] = (lo, hi)
        self.ov[key] = ov
        for k in ov:
            self.ov[k].append(key)

    def _deps(self, eng, reads, writes):
        deps = {}

        def add(evs):
            for (k, v) in evs:
                if deps.get(k, 0) < v:
                    deps[k] = v
        for k in reads:
            r = self.res.get(k)
            if r:
                add(r[0])
            for k2 in self.ov.get(k, ()):
                r = self.res.get(k2)
                if r:
                    add(r[0])
        for k in writes:
            r = self.res.get(k)
            if r:
                add(r[0]); add(r[1])
            for k2 in self.ov.get(k, ()):
                r = self.res.get(k2)
                if r:
                    add(r[0]); add(r[1])
        waits = []
        for k, v in deps.items():
            if k == eng and (eng == 'pe' or not SAME_ENG_SYNC):
                continue
            if self.seen[eng].get(k, 0) >= v:
                continue
            self.seen[eng][k] = v
            waits.append((k, v))
        return waits

    def _update(self, ev, reads, writes):
        for k in writes:
            self.res[k] = ([ev], [])
        for k in reads:
            r = self.res.setdefault(k, ([], []))
            lst = [e for e in r[1] if e[0] != ev[0]]
            lst.append(ev)
            self.res[k] = (r[0], lst)

    def op(self, eng, fn, reads=(), writes=(), sig=True):
        if self.capture is not None:
            self.capture.append(('op', (eng, fn, tuple(reads), tuple(writes), sig)))
            return None
        waits = self._deps(eng, reads, writes)
        if sig:
            self.cnt[eng] += 1
            ev = (eng, self.cnt[eng])
        else:
            ev = (eng, self.cnt[eng] + 1)
        self.ops[eng].append((waits, fn, eng if sig else None, 1))
        self._update(ev, reads, writes)
        return ev

    def dma(self, q, out, in_, reads=(), writes=(), sem=None, **kw):
        if self.capture is not None:
            self.capture.append(('dma', (q, out, in_, tuple(reads), tuple(writes), sem, kw)))
            return None
        if sem is None:
            sem = 'd_' + str(writes[0] if writes else reads[0])
        if sem not in self.dsem:
            self.dsem[sem] = [self.stack.enter_context(self.nc.semaphore("q%d" % len(self.dsem))), 0]
        waits = self._deps(q, reads, writes)
        self.dsem[sem][1] += 16
        ev = (sem, self.dsem[sem][1])
        self.ops[q].append((waits, (lambda e, o=out, i=in_, kw=kw: e.dma_start(out=o, in_=i, **kw)), sem, 16))
        self._update(ev, reads, writes)
        return ev

    def wait_sem_final(self, eng, semname):
        self.ops[eng].append(([(semname, self.dsem[semname][1])], None, None, 0))

    def _semof(self, k):
        return self.sems[k] if k in self.sems else self.dsem[k][0]

    def emit(self):
        nc = self.nc
        with nc.Block() as block:
            def run(e, name):
                for (waits, fn, inc, amt) in self.ops[name]:
                    for (k, v) in waits:
                        e.wait_ge(self._semof(k), v)
                    if fn is None:
                        continue
                    ins = fn(e)
                    if inc is not None:
                        ins.then_inc(self._semof(inc), amt)

            @block.tensor
            def _(e):
                run(e, 'pe')

            @block.scalar
            def _(e):
                run(e, 'act')

            @block.vector
            def _(e):
                run(e, 'dve')

            @block.gpsimd
            def _(e):
                run(e, 'pool')

            @block.sync
            def _(e):
                run(e, 'sp')


DTSZ = {F32: 4, BF16: 2, U8: 1}


class Arena:
    def __init__(self, S, tensor, size):
        self.S = S
        self.t = tensor
        self.size = size
        self.off = 0
        self.phase = 0

    def reset(self):
        self.off = 0
        self.phase += 1

    def alloc(self, key, free_shape, dt):
        n = int(np.prod(free_shape)) * DTSZ[dt]
        off = (self.off + 31) // 32 * 32
        assert off + n <= self.size, (key, off, n, self.size)
        self.off = off + n
        ap = self.t[:, off:off + n].bitcast(dt)
        if len(free_shape) > 1:
            names = "abcdef"[:len(free_shape)]
            kw = {nm: int(s) for nm, s in zip(names, free_shape)}
            ap = ap.rearrange("p (%s) -> p %s" % (" ".join(names), " ".join(names)), **kw)
        self.S.register(key, off, off + n)
        return ap


def build_program(stages=(1, 2, 3, 4)):
    stages = tuple(stages)
    nc = bass.Bass("TRN2", target_bir_lowering=False)

    def din(name, shape, dt=F32):
        return nc.dram_tensor(name, list(shape), dt, kind="ExternalInput").ap()

    x_d = din("x", [NT, D])
    xh_d = din("xh", [256, D])
    cT_d = din("cT", [128, KC])
    ident_d = din("ident", [128, 128])
    ada_w_d = din("ada_w", [2, D, 6 * D])
    ada_b_d = din("ada_b", [2, 6 * D])
    ln_g_d = din("ln_g", [2, 2, D])
    ln_b_d = din("ln_b", [2, 2, D])
    ev_w_in_d = din("ev_w_in", [D, 1280])
    ev_pool_w_d = din("ev_pool_w", [4, 128, 128])
    pscale_d = din("pscaleT", [128, 4])
    sink_d = din("ev_sink", [1, 8])
    ev_w_out_d = din("ev_w_out", [D, D])
    band_d = din("band", [128, 7, 4, 128])
    rc_d = din("rc", [1, 2 * 4 * 128])
    dist_d = din("dist", [128, 384])
    dmask_d = din("dmask", [128, 3, 384])
    od_w_in_d = din("od_w_in", [D, 2 * D])
    sg_g_d = din("od_sg_ln_g", [1, D])
    sg_b_d = din("od_sg_ln_b", [1, D])
    wsT_d = din("wsT", [128, 8, 128])
    bsT_d = din("bsT", [128, 8])
    od_w_out_d = din("od_w_out", [D, D])
    ffn_wg_d = din("ffn_w_gate", [D, D_FF])
    ffn_wu_d = din("ffn_w_up", [D, D_FF])
    ffn_wd_d = din("ffn_w_down", [D_FF, D])
    if 4 in stages:
        wr_d = din("moe_w_router", [D, NEXP])
        moe_wg_d = din("moe_w_gate", [NEXP_DECL, D, D_FFE])
        moe_wu_d = din("moe_w_up", [NEXP_DECL, D, D_FFE])
        moe_wd_d = din("moe_w_down", [NEXP_DECL, D_FFE, D])
    out_d = nc.dram_tensor("out", [NT, D], F32, kind="ExternalOutput").ap()
    modscr = nc.dram_tensor("modscr", [2, 6 * D], F32, kind="Internal").ap()

    with ExitStack() as st:
        S = Sched(nc, st)

        def sb(name, shape, dt):
            return st.enter_context(nc.sbuf_tensor(name, shape, dt))

        X = sb("X", [128, NTILE, D], F32)
        hT = sb("hT", [128, KC, NT], BF16)
        ident = sb("ident32", [128, 128], F32)
        identb = sb("identb", [128, 128], BF16)
        tabs = sb("tabs", [128, 2, 4, KC], F32)
        gate = sb("gate", [128, NTILE, NEXP], F32)
        lnbc = sb("lnbc", [128, 2, D], F32)
        gp1 = sb("gp1", [128, D], F32)
        tmpA = sb("tmpA", [128, D], F32)
        small = sb("small", [128, 64], F32)
        epsb = sb("epsb", [128, 1], F32)
        arena_t = sb("arena", [128, ARENA], U8)
        AR = Arena(S, arena_t, ARENA)
        ps = st.enter_context(nc.psum_tensor("ps", [128, 8, 512], F32))

        def PS(b):
            return 'ps%d' % b
        for b_ in range(8):
            S.register('ps%d' % b_, 10 ** 6 + 2048 * b_, 10 ** 6 + 2048 * b_ + 2048)
            for h_ in range(2):
                S.register('ps%dh%d' % (b_, h_), 10 ** 6 + 2048 * b_ + 1024 * h_, 10 ** 6 + 2048 * b_ + 1024 * h_ + 1024)

        def ps2(b):
            return ps[:, b:b + 2, :].rearrange("p a b -> p (a b)")

        def w_view(w2d):
            return w2d.rearrange("(c p) n -> p c n", p=128)

        S.dma('sp', ident[:], ident_d, writes=['ident'])
        S.op('dve', lambda e: e.tensor_copy(out=identb[:], in_=ident[:]), reads=['ident'], writes=['identb'])
        S.op('dve', lambda e: e.memset(epsb[:], LN_EPS), writes=['epsb'])
        for g4 in range(4):
            S.dma('sp', X[:, 4 * g4:4 * g4 + 4, :], x_d[512 * g4:512 * g4 + 512, :].rearrange("(j p) d -> p j d", p=128),
                  writes=[('X', 4 * g4 + i) for i in range(4)], sem='xin%d' % g4)

        AR.reset()
        cT = AR.alloc('p_cT', [KC], F32)
        cond_rep = AR.alloc('p_condrep', [KC, 128], F32)
        mod_bc = AR.alloc('p_modbc', [6 * D], F32)
        adaw_sl = [AR.alloc('p_adaw%d' % i, [KC, 512], F32) for i in range(3)]
        adab_sl = [AR.alloc('p_adab%d' % i, [512], F32) for i in range(3)]
        S.dma('sp', cT, cT_d, writes=['p_cT'])
        S.op('act', lambda e: e.activation(out=cT, in_=cT, func=AF.Silu), reads=['p_cT'], writes=['p_cT'])
        S.op('dve', lambda e: e.tensor_copy(out=cond_rep, in_=cT.unsqueeze(2).to_broadcast([128, KC, 128])),
             reads=['p_cT'], writes=['p_condrep'])
        it = 0
        for l in range(2):
            for blk in range(12):
                sl = it % 3
                bank = it % 4
                it += 1
                S.dma('sp', adaw_sl[sl], w_view(ada_w_d[l])[:, :, 512 * blk:512 * blk + 512], writes=['p_adaw%d' % sl])
                S.dma('sp', adab_sl[sl], ada_b_d[l:l + 1, 512 * blk:512 * blk + 512].partition_broadcast(128),
                      writes=['p_adab%d' % sl])
                for k in range(KC):
                    S.op('pe', lambda e, k=k, sl=sl, bank=bank: e.matmul(ps[:, bank, :], lhsT=cond_rep[:, k, :], rhs=adaw_sl[sl][:, k, :],
                                                                        start=(k == 0), stop=(k == KC - 1)),
                         reads=['p_condrep', 'p_adaw%d' % sl], writes=[PS(bank)], sig=(k == KC - 1))
                S.op('dve', lambda e, sl=sl, bank=bank, blk=blk: e.tensor_tensor(out=mod_bc[:, 512 * blk:512 * blk + 512], in0=ps[:, bank, :],
                                                                              in1=adab_sl[sl], op=ALU.add),
                     reads=[PS(bank), 'p_adab%d' % sl], writes=['p_modbc'])
            for v in (1, 2, 4, 5):
                S.op('dve', lambda e, v=v: e.tensor_scalar_add(out=mod_bc[:, v * D:(v + 1) * D], in0=mod_bc[:, v * D:(v + 1) * D], scalar1=1.0),
                     reads=['p_modbc'], writes=['p_modbc'])
            S.dma('sp', modscr[l:l + 1, :], mod_bc[0:1, :], reads=['p_modbc'], writes=['modscr'], sem='modw')
            for ti, v in enumerate((1, 0, 4, 3)):
                for half in range(2):
                    bank = 4 + (ti * 2 + half) % 4
                    for kk in range(4):
                        k = half * 4 + kk
                        S.op('pe', lambda e, v=v, k=k, kk=kk, bank=bank: e.transpose(ps[:, bank, 128 * kk:128 * kk + 128],
                                                                                   mod_bc[:, v * D + 128 * k: v * D + 128 * k + 128], ident[:]),
                             reads=['p_modbc', 'ident'], writes=[PS(bank)], sig=(kk == 3))
                    S.op('dve', lambda e, l=l, ti=ti, half=half, bank=bank: e.tensor_copy(
                        out=tabs[:, l, ti, 4 * half:4 * half + 4].unsqueeze(2),
                        in_=ps[:, bank, :].rearrange("p (a b) -> p a b", a=4)[:, :, 0:1]),
                        reads=[PS(bank)], writes=['tabs'])

        def load_ln_params(l, which):
            S.dma('sp', lnbc[:, 0, :], ln_g_d[l, which:which + 1, :].partition_broadcast(128), writes=['lnbc'], sem='lnbc')
            S.dma('sp', lnbc[:, 1, :], ln_b_d[l, which:which + 1, :].partition_broadcast(128), writes=['lnbc'], sem='lnbc')

        def load_gp1(l, which):
            blk = 2 if which == 0 else 5
            S.dma('sp', gp1[:], modscr[l:l + 1, blk * D:(blk + 1) * D].partition_broadcast(128), reads=['modscr'], writes=['gp1'])

        ln_rr = [0]

        def emit_ln(j):
            xk = ('X', j)
            o = 16 * (ln_rr[0] % 2)
            tg = 'ln%d' % (ln_rr[0] % 2)
            ln_rr[0] += 1
            st6 = small[:, o:o + 12].rearrange("p (a b) -> p a b", a=2)
            mv = small[:, o + 12:o + 14]
            rstd = small[:, o + 14:o + 15]
            nmr = small[:, o + 15:o + 16]
            for hlf in range(2):
                S.op('dve', lambda e, hlf=hlf: e.bn_stats(out=st6[:, hlf, :], in_=X[:, j, 512 * hlf:512 * hlf + 512]),
                     reads=[xk], writes=[tg + 'st'])
            S.op('dve', lambda e: e.bn_aggr(out=mv, in_=small[:, o:o + 12]), reads=[tg + 'st'], writes=[tg + 'mv'])
            S.op('act', lambda e: e.activation(out=rstd, in_=mv[:, 1:2], func=AF.Ln, bias=epsb[:, 0:1], scale=1.0),
                 reads=[tg + 'mv', 'epsb'], writes=[tg + 'rs'])
            S.op('act', lambda e: e.activation(out=rstd, in_=rstd, func=AF.Exp, scale=-0.5), reads=[tg + 'rs'], writes=[tg + 'rs'])
            S.op('dve', lambda e: e.tensor_scalar(out=nmr, in0=mv[:, 0:1], scalar1=rstd, scalar2=-1.0, op0=ALU.mult, op1=ALU.mult),
                 reads=[tg + 'mv', tg + 'rs'], writes=[tg + 'nm'])
            S.op('act', lambda e: e.activation(out=X[:, j, :], in_=X[:, j, :], func=AF.Identity, bias=nmr, scale=rstd),
                 reads=[xk, tg + 'rs', tg + 'nm'], writes=[xk])
            S.op('dve', lambda e: e.tensor_tensor(out=X[:, j, :], in0=X[:, j, :], in1=lnbc[:, 0, :], op=ALU.mult),
                 reads=[xk, 'lnbc'], writes=[xk])
            S.op('dve', lambda e: e.tensor_tensor(out=X[:, j, :], in0=X[:, j, :], in1=lnbc[:, 1, :], op=ALU.add),
                 reads=[xk, 'lnbc'], writes=[xk])

        tr_rr = [0]

        def emit_hT(src, src_keys, dst_fn, dst_keys, l, ti, banks=(0, 1), router=None):
            for half in range(2):
                bank = banks[half]
                for kk in range(4):
                    k = half * 4 + kk
                    S.op('pe', lambda e, k=k, kk=kk, bank=bank: e.transpose(ps[:, bank, 128 * kk:128 * kk + 128], src[:, 128 * k:128 * k + 128], ident[:]),
                         reads=list(src_keys) + ['ident'], writes=[PS(bank)], sig=(kk == 3))
                for kk in range(4):
                    k = half * 4 + kk
                    if router is None:
                        S.op('act', lambda e, k=k, kk=kk, bank=bank: e.activation(out=dst_fn(k), in_=ps[:, bank, 128 * kk:128 * kk + 128], func=AF.Identity,
                                                                               bias=tabs[:, l, ti + 1, k:k + 1], scale=tabs[:, l, ti, k:k + 1]),
                             reads=[PS(bank), 'tabs'], writes=list(dst_keys))
                    else:
                        h32, h32key = router
                        S.op('act', lambda e, k=k, kk=kk, bank=bank: e.activation(out=h32[:, k, :], in_=ps[:, bank, 128 * kk:128 * kk + 128], func=AF.Identity,
                                                                               bias=tabs[:, l, ti + 1, k:k + 1], scale=tabs[:, l, ti, k:k + 1]),
                             reads=[PS(bank), 'tabs'], writes=[h32key])
                        S.op('dve', lambda e, k=k: e.tensor_copy(out=dst_fn(k), in_=h32[:, k, :]), reads=[h32key], writes=list(dst_keys))

        def hT_own(j):
            return (lambda k: hT[:, k, 128 * j:128 * j + 128])

        def stage1():
            AR.reset()
            winp = AR.alloc('a_winp', [KC, 512], BF16)
            winq = AR.alloc('a_winq', [KC, 512], BF16)
            wink = AR.alloc('a_wink', [KC, 2, 128], BF16)
            winv = AR.alloc('a_winv', [KC, 128], BF16)
            wout = AR.alloc('a_wout', [KC, D], BF16)
            stg = tmpA[:]
            wpool = AR.alloc('a_wpool', [4, 128], BF16)
            band = AR.alloc('a_band', [7, 4, 128], BF16)
            rc = AR.alloc('a_rc', [2, 4, 128], F32)
            dist = AR.alloc('a_dist', [384], F32)
            dmask = AR.alloc('a_dmask', [3, 384], F32)
            sinkb = AR.alloc('a_sink', [8], F32)
            pscale = AR.alloc('a_pscale', [4], F32)
            xh = tmpA[:]
            hTh = AR.alloc('a_hTh', [KC, 128], BF16)
            p_ring = [AR.alloc('a_p%d' % i, [512], BF16) for i in range(4)]
            v_ring = [AR.alloc('a_v%d' % i, [128], BF16) for i in range(4)]
            k_ring = [AR.alloc('a_k%d' % i, [2, 128], BF16) for i in range(4)]
            q_buf = [AR.alloc('a_q%d' % i, [4, 128], BF16) for i in range(4)]
            s_sbP = [[AR.alloc('a_s%d_%d' % (pp, i), [384], F32) for i in range(2)] for pp in range(2)]
            p_bfP = [[AR.alloc('a_pb%d_%d' % (pp, i), [384], BF16) for i in range(2)] for pp in range(2)]
            pTP = [[AR.alloc('a_pT%d_%d' % (pp, i), [3, 128], BF16) for i in range(2)] for pp in range(2)]
            o_sbP = [AR.alloc('a_osb%d' % pp, [512], BF16) for pp in range(2)]
            mixTP = [AR.alloc('a_mixT%d' % pp, [8, 128], BF16) for pp in range(2)]
            poolTP = [AR.alloc('a_poolT%d' % pp, [4, 128], BF16) for pp in range(2)]
            attP = [AR.alloc('a_att%d' % pp, [64], F32) for pp in range(2)]

            wv = w_view(ev_w_in_d)
            S.dma('pool', winp, wv[:, :, 0:512], writes=['a_winp'])
            S.dma('pool', winq, wv[:, :, 512:1024], writes=['a_winq'])
            for g in range(2):
                for dup in range(2):
                    S.dma('pool', wink[:, :, g, 64 * dup:64 * dup + 64], wv[:, :, 1024 + 64 * g:1024 + 64 * g + 64], writes=['a_wink'], sem='a_wink')
            S.dma('pool', winv, wv[:, :, 1152:1280], writes=['a_winv'])
            S.dma('pool', wpool, ev_pool_w_d.rearrange("g c d -> c g d"), writes=['a_wpool'])
            S.dma('pool', band, band_d, writes=['a_band'])
            S.dma('sp', rc, rc_d.partition_broadcast(128).rearrange("p o (a b c) -> p (o a) b c", a=2, b=4), writes=['a_rc'])
            S.dma('sp', dist, dist_d, writes=['a_dist'])
            S.dma('sp', dmask, dmask_d, writes=['a_dmask'])
            S.dma('sp', sinkb, sink_d.partition_broadcast(128).rearrange("p o e -> p (o e)"), writes=['a_sink'])
            S.dma('sp', pscale, pscale_d, writes=['a_pscale'])
            load_gp1(0, 0)
            load_ln_params(0, 0)
            wo_v = w_view(ev_w_out_d)
            for c in range(KC):
                S.dma('sp', stg, wo_v[:, c, :], writes=['tmpA'])
                S.op('dve', lambda e, c=c: e.tensor_tensor(out=wout[:, c, :], in0=stg, in1=gp1[:], op=ALU.mult),
                     reads=['tmpA', 'gp1'], writes=['a_wout'])

            def proj_tile(e_idx):
                slot = e_idx % 4
                own = 1 <= e_idx <= 16
                if own:
                    j = e_idx - 1
                    emit_hT(X[:, j, :], [('X', j)], hT_own(j), [('hT', j)], 0, 0, banks=(0, 0))
                    hsrc = lambda k: hT[:, k, 128 * j:128 * j + 128]
                    hkey = ('hT', j)
                else:
                    r0 = 0 if e_idx == 0 else 128
                    S.dma('sp', xh, xh_d[r0:r0 + 128, :], writes=['tmpA'])
                    emit_hT(xh, ['tmpA'], (lambda k: hTh[:, k, :]), ['a_hTh'], 0, 0, banks=(0, 0))
                    hsrc = lambda k: hTh[:, k, :]
                    hkey = 'a_hTh'
                for k in range(KC):
                    S.op('pe', lambda e, k=k: e.matmul(ps[:, 1, :], lhsT=hsrc(k), rhs=winp[:, k, :], start=(k == 0), stop=(k == KC - 1)),
                         reads=[hkey, 'a_winp'], writes=[PS(1)], sig=(k == KC - 1))
                S.op('act', lambda e: e.activation(out=p_ring[slot], in_=ps[:, 1, :], func=AF.Identity), reads=[PS(1)], writes=['a_p%d' % slot])
                for k in range(KC):
                    S.op('pe', lambda e, k=k: e.matmul(ps[:, 2, 0:128], lhsT=hsrc(k), rhs=winv[:, k, :], start=(k == 0), stop=(k == KC - 1)),
                         reads=[hkey, 'a_winv'], writes=[PS(2)], sig=False)
                for g in range(2):
                    for k in range(KC):
                        S.op('pe', lambda e, k=k, g=g: e.matmul(ps[:, 2, 128 + 128 * g:256 + 128 * g], lhsT=wink[:, k, g, :], rhs=hsrc(k),
                                                               start=(k == 0), stop=(k == KC - 1)),
                             reads=[hkey, 'a_wink'], writes=[PS(2)], sig=(g == 1 and k == KC - 1))
                S.op('dve', lambda e: e.tensor_copy(out=v_ring[slot], in_=ps[:, 2, 0:128]), reads=[PS(2)], writes=['a_v%d' % slot])
                S.op('dve', lambda e: e.tensor_copy(out=k_ring[slot], in_=ps[:, 2, 128:384].rearrange("p (a b) -> p a b", a=2)),
                     reads=[PS(2)], writes=['a_k%d' % slot])
                if own:
                    qs = (e_idx - 1) % 4
                    for c in range(4):
                        for k in range(KC):
                            S.op('pe', lambda e, k=k, c=c: e.matmul(ps[:, 1, 128 * c:128 * c + 128], lhsT=winq[:, k, 128 * c:128 * c + 128], rhs=hsrc(k),
                                                                   start=(k == 0), stop=(k == KC - 1)),
                                 reads=[hkey, 'a_winq'], writes=[PS(1)], sig=(c == 3 and k == KC - 1))
                    S.op('act', lambda e: e.activation(out=q_buf[qs], in_=ps[:, 1, :].rearrange("p (a b) -> p a b", a=4), func=AF.Identity),
                         reads=[PS(1)], writes=['a_q%d' % qs])

            def mix_tile(j, proj_ops=None):
                par = j % 2
                s_sb, p_bf, pT, o_sb, mixT, poolT, att = s_sbP[par], p_bfP[par], pTP[par], o_sbP[par], mixTP[par], poolTP[par], attP[par]
                kmix, kpool, kosb = 'a_mixT%d' % par, 'a_poolT%d' % par, 'a_osb%d' % par
                akeys = [('att', par, h) for h in range(8)]
                var = 0 if j == 0 else (2 if j == NTILE - 1 else 1)
                bsel = {0: (3, 4, 2), 1: (0, 1, 2), 2: (0, 5, 6)}[var]
                for g in range(4):
                    for i3 in range(3):
                        sl = (j + i3) % 4
                        S.op('pe', lambda e, g=g, i3=i3, sl=sl: e.matmul(ps[:, 4, 128 * g:128 * g + 128], lhsT=p_ring[sl][:, 128 * g:128 * g + 128],
                                                                        rhs=band[:, bsel[i3], g, :], start=(i3 == 0), stop=(i3 == 2)),
                             reads=['a_p%d' % sl, 'a_band'], writes=[PS(4)], sig=(g == 3 and i3 == 2))
                if var == 1:
                    for g in range(4):
                        S.op('act', lambda e, g=g: e.activation(out=poolT[:, g, :], in_=ps[:, 4, 128 * g:128 * g + 128], func=AF.Identity,
                                                               scale=1.0 / (POOL_WINDOWS[g] + 1)),
                             reads=[PS(4)], writes=[kpool])
                else:
                    ri = 0 if var == 0 else 1
                    S.op('dve', lambda e: e.tensor_tensor(out=poolT, in0=ps[:, 4, :].rearrange("p (a b) -> p a b", a=4), in1=rc[:, ri, :, :], op=ALU.mult),
                         reads=[PS(4), 'a_rc'], writes=[kpool])
                for g in range(4):
                    S.op('pe', lambda e, g=g: e.matmul(ps[:, 4, 128 * g:128 * g + 128], lhsT=wpool[:, g, :], rhs=poolT[:, g, :], start=True, stop=True),
                         reads=['a_wpool', kpool], writes=[PS(4)], sig=(g == 3))
                for g in range(4):
                    S.op('act', lambda e, g=g: e.activation(out=mixT[:, g, :], in_=ps[:, 4, 128 * g:128 * g + 128], func=AF.Identity, scale=pscale[:, g:g + 1]),
                         reads=[PS(4), 'a_pscale'], writes=[kmix])
                qs = j % 4
                S.op('dve', lambda e: e.memset(att[:, 16:24], 0.0), writes=akeys)

                def hvars(h):
                    c, hp, g = h // 2, h % 2, h // 4
                    sb_i = h % 2
                    return dict(c=c, g=g, sb_i=sb_i, bank=6 + (h % 2), r0=64 * hp, ak=('att', par, h),
                                sk='a_s%d_%d' % (par, sb_i), pk='a_pb%d_%d' % (par, sb_i), tk='a_pT%d_%d' % (par, sb_i),
                                tbank=(3 if h % 2 == 0 else 5))

                def headA(h):
                    v_ = hvars(h)
                    c, g, bank, r0 = v_['c'], v_['g'], v_['bank'], v_['r0']
                    for i3 in range(3):
                        sl = (j + i3) % 4
                        S.op('pe', lambda e, i3=i3, sl=sl: e.matmul(
                            ps[:, bank, 128 * i3:128 * i3 + 128], lhsT=q_buf[qs][r0:r0 + 64, c, :], rhs=k_ring[sl][r0:r0 + 64, g, :], start=True, stop=True),
                            reads=['a_q%d' % qs, 'a_k%d' % sl], writes=[PS(bank)], sig=(i3 == 2))

                def headB(h):
                    v_ = hvars(h)
                    sb_i, bank, ak, sk, pk = v_['sb_i'], v_['bank'], v_['ak'], v_['sk'], v_['pk']
                    S.op('dve', lambda e: e.scalar_tensor_tensor(out=s_sb[sb_i], in0=ps[:, bank, 0:384], scalar=0.125, in1=dmask[:, var, :],
                                                                 op0=ALU.mult, op1=ALU.add),
                         reads=[PS(bank), 'a_dmask'], writes=[sk])
                    S.op('dve', lambda e: e.scalar_tensor_tensor(out=s_sb[sb_i], in0=dist, scalar=-SLOPES[h], in1=s_sb[sb_i],
                                                                 op0=ALU.mult, op1=ALU.add),
                         reads=['a_dist', sk], writes=[sk])
                    S.op('dve', lambda e: e.reduce_max(out=att[:, h:h + 1], in_=s_sb[sb_i], axis=AX.X), reads=[sk], writes=[ak])
                    S.op('dve', lambda e: e.tensor_scalar(out=att[:, 8 + h:9 + h], in0=att[:, h:h + 1], scalar1=sinkb[:, h:h + 1], scalar2=-1.0,
                                                          op0=ALU.max, op1=ALU.mult),
                         reads=[ak, 'a_sink'], writes=[ak])
                    S.op('act', lambda e: e.activation(out=p_bf[sb_i], in_=s_sb[sb_i], func=AF.Exp, bias=att[:, 8 + h:9 + h], scale=1.0,
                                                       accum_out=att[:, 16 + h:17 + h]),
                         reads=[sk, ak], writes=[pk, ak])
                    S.op('act', lambda e: e.activation(out=att[:, 24 + h:25 + h], in_=sinkb[:, h:h + 1], func=AF.Exp, bias=att[:, 8 + h:9 + h], scale=1.0),
                         reads=['a_sink', ak], writes=[ak])

                def headC(h):
                    v_ = hvars(h)
                    sb_i, pk, tk, tbank = v_['sb_i'], v_['pk'], v_['tk'], v_['tbank']
                    psb = ps[:, tbank, :].bitcast(BF16)[:, 0:384]
                    for i3 in range(3):
                        S.op('pe', lambda e, i3=i3: e.transpose(psb[:, 128 * i3:128 * i3 + 128], p_bf[sb_i][:, 128 * i3:128 * i3 + 128], identb[:]),
                             reads=[pk, 'identb'], writes=[PS(tbank)], sig=(i3 == 2))
                    S.op('dve', lambda e: e.tensor_copy(out=pT[sb_i], in_=psb.rearrange("p (a b) -> p a b", a=3)),
                         reads=[PS(tbank)], writes=[tk])

                def headD(h):
                    v_ = hvars(h)
                    sb_i, g, tk = v_['sb_i'], v_['g'], v_['tk']
                    for i3 in range(3):
                        sl = (j + i3) % 4
                        S.op('pe', lambda e, i3=i3, sl=sl: e.matmul(ps[:, 4, 64 * h:64 * h + 64], lhsT=pT[sb_i][:, i3, :],
                                                                   rhs=v_ring[sl][:, 64 * g:64 * g + 64], start=(i3 == 0), stop=(i3 == 2)),
                             reads=[tk, 'a_v%d' % sl], writes=[PS(4)], sig=(i3 == 2))

                S.pipeline([headA, headB, headC, headD], 8, extra=proj_ops)
                S.op('dve', lambda e: e.tensor_tensor(out=att[:, 32:40], in0=att[:, 16:24], in1=att[:, 24:32], op=ALU.add), reads=akeys, writes=akeys)
                S.op('dve', lambda e: e.reciprocal(out=att[:, 32:40], in_=att[:, 32:40]), reads=akeys, writes=akeys)
                S.op('dve', lambda e: e.tensor_tensor(out=o_sb.rearrange("p (a b) -> p a b", a=8), in0=ps[:, 4, :].rearrange("p (a b) -> p a b", a=8),
                                                      in1=att[:, 32:40].unsqueeze(2).to_broadcast([128, 8, 64]), op=ALU.mult),
                     reads=[PS(4)] + akeys, writes=[kosb])
                psb0 = ps[:, 3, :].bitcast(BF16)
                for c in range(4):
                    S.op('pe', lambda e, c=c: e.transpose(psb0[:, 128 * c:128 * c + 128], o_sb[:, 128 * c:128 * c + 128], identb[:]),
                         reads=[kosb, 'identb'], writes=[PS(3)], sig=(c == 3))
                S.op('dve', lambda e: e.tensor_copy(out=mixT[:, 4:8, :], in_=psb0[:, 0:512].rearrange("p (a b) -> p a b", a=4)),
                     reads=[PS(3)], writes=[kmix])
                for n in range(2):
                    for c in range(KC):
                        S.op('pe', lambda e, n=n, c=c: e.matmul(ps[:, 6 + n, :], lhsT=mixT[:, c, :], rhs=wout[:, c, 512 * n:512 * n + 512],
                                                               start=(c == 0), stop=(c == KC - 1)),
                             reads=[kmix, 'a_wout'], writes=[PS(6 + n)], sig=(c == KC - 1))
                S.op('dve', lambda e: e.scalar_tensor_tensor(out=X[:, j, :], in0=X[:, j, :], scalar=ALPHA, in1=ps2(6), op0=ALU.mult, op1=ALU.add),
                     reads=[('X', j), PS(6), PS(7)], writes=[('X', j)])
                emit_ln(j)

            for e_idx in range(3):
                proj_tile(e_idx)
            for j in range(NTILE):
                mix_tile(j, S.cap(proj_tile, j + 3) if j + 3 < 18 else None)

        def make_hT_all(l, ti, router_ctx=None):
            def one(j):
                if router_ctx is None:
                    emit_hT(X[:, j, :], [('X', j)], hT_own(j), [('hT', j)], l, ti, banks=((0, 1) if j % 2 == 0 else (2, 3)))
                else:
                    router_ctx(j)
            S.interleave([S.cap(one, j) for j in range(NTILE)], W=2)

        def stage_ffn(l, moe):
            AR.reset()
            pre = 'f%d_' % l
            wg_sl = [AR.alloc(pre + 'wg%d' % i, [KC, GCH * 128], BF16) for i in range(2)]
            wu_sl = [AR.alloc(pre + 'wu%d' % i, [KC, GCH * 128], BF16) for i in range(2)]
            wd_sl = [AR.alloc(pre + 'wd%d' % i, [GCH, D], BF16) for i in range(2)]
            stg = AR.alloc(pre + 'stg', [GCH, D], F32)
            aT = [AR.alloc(pre + 'aT%d' % i, [GCH, 512], BF16) for i in range(2)]
            sg = [AR.alloc(pre + 'sg%d' % i, [512], F32) for i in range(2)]
            load_gp1(l, 1)
            load_ln_params(l, 1)
            if moe:
                h32b = AR.alloc(pre + 'h32b', [KC, 128], F32)
                h32P = [(tmpA[:].rearrange("p (a b) -> p a b", a=KC), 'tmpA'), (h32b, pre + 'h32b')]
                wr = AR.alloc(pre + 'wr', [KC, NEXP], F32)
                rtP = [AR.alloc(pre + 'rt%d' % i, [64], F32) for i in range(2)]
                S.dma('sp', wr, w_view(wr_d), writes=[pre + 'wr'])

                def router_ctx(j):
                    par = j % 2
                    h32, hkey32 = h32P[par]
                    rt = rtP[par]
                    rb = 6 + par
                    emit_hT(X[:, j, :], [('X', j)], hT_own(j), [('hT', j)], l, 2, banks=((0, 1) if par == 0 else (2, 3)), router=(h32, hkey32))
                    for k in range(KC):
                        S.op('pe', lambda e, k=k: e.matmul(ps[:, rb, 0:NEXP], lhsT=h32[:, k, :], rhs=wr[:, k, :], start=(k == 0), stop=(k == KC - 1)),
                             reads=[hkey32, pre + 'wr'], writes=[PS(rb)], sig=(k == KC - 1))
                    lg, eq, l2, ex = rt[:, 0:8], rt[:, 8:16], rt[:, 16:24], rt[:, 24:32]
                    m1, m2, nm1, den = rt[:, 32:33], rt[:, 33:34], rt[:, 34:35], rt[:, 35:36]
                    rk = pre + 'rt%d' % par
                    S.op('dve', lambda e: e.tensor_copy(out=lg, in_=ps[:, rb, 0:NEXP]), reads=[PS(rb)], writes=[rk])
                    S.op('dve', lambda e: e.reduce_max(out=m1, in_=lg, axis=AX.X), reads=[rk], writes=[rk])
                    S.op('dve', lambda e: e.tensor_scalar(out=eq, in0=lg, scalar1=m1, scalar2=None, op0=ALU.is_equal), reads=[rk], writes=[rk])
                    S.op('dve', lambda e: e.scalar_tensor_tensor(out=l2, in0=eq, scalar=-1e30, in1=lg, op0=ALU.mult, op1=ALU.add), reads=[rk], writes=[rk])
                    S.op('dve', lambda e: e.reduce_max(out=m2, in_=l2, axis=AX.X), reads=[rk], writes=[rk])
                    S.op('dve', lambda e: e.tensor_scalar(out=eq, in0=lg, scalar1=m2, scalar2=None, op0=ALU.is_ge), reads=[rk], writes=[rk])
                    S.op('dve', lambda e: e.tensor_scalar_mul(out=nm1, in0=m1, scalar1=-1.0), reads=[rk], writes=[rk])
                    S.op('act', lambda e: e.activation(out=ex, in_=lg, func=AF.Exp, bias=nm1, scale=1.0), reads=[rk], writes=[rk])
                    S.op('dve', lambda e: e.tensor_tensor(out=ex, in0=ex, in1=eq, op=ALU.mult), reads=[rk], writes=[rk])
                    S.op('dve', lambda e: e.reduce_sum(out=den, in_=ex, axis=AX.X), reads=[rk], writes=[rk])
                    S.op('dve', lambda e: e.reciprocal(out=den, in_=den), reads=[rk], writes=[rk])
                    S.op('dve', lambda e: e.tensor_scalar(out=gate[:, j, :], in0=ex, scalar1=den, scalar2=None, op0=ALU.mult), reads=[rk], writes=[('gate', j)])
                if DBG_NOROUTER:
                    make_hT_all(l, 2)
                    for j in range(NTILE):
                        S.op('dve', lambda e, j=j: e.memset(gate[:, j, :], 0.125), writes=[('gate', j)])
                else:
                    make_hT_all(l, 2, router_ctx)
                experts = [(moe_wg_d[e_], moe_wu_d[e_], moe_wd_d[e_]) for e_ in range(NEXP_DECL)]
                ff = D_FFE
            else:
                make_hT_all(l, 2)
                experts = [(ffn_wg_d, ffn_wu_d, ffn_wd_d)]
                ff = D_FF
            if moe:
                for j in range(NTILE):
                    S.op('act', lambda e, j=j: e.activation(out=X[:, j, :], in_=X[:, j, :], func=AF.Identity, scale=ALPHA),
                         reads=[('X', j)], writes=[('X', j)])

            nch = ff // 128
            groups = []
            for ei in range(len(experts)):
                c0 = 0
                while c0 < nch:
                    gc = min(GCH, nch - c0)
                    groups.append((ei, c0, gc))
                    c0 += gc

            import os
            if os.environ.get("KDBG_NG"):
                groups = groups[:int(os.environ["KDBG_NG"])]

            def load_group(gi):
                ei, c0, gc = groups[gi]
                sl = gi % 2
                wg_d, wu_d, wd_d = experts[ei]
                f0, f1 = 128 * c0, 128 * (c0 + gc)
                S.dma('pool', wg_sl[sl][:, :, 0:128 * gc], w_view(wg_d)[:, :, f0:f1], writes=[pre + 'wg%d' % sl])
                S.dma('pool', wu_sl[sl][:, :, 0:128 * gc], w_view(wu_d)[:, :, f0:f1], writes=[pre + 'wu%d' % sl])
                S.dma('sp', stg[:, 0:gc, :], wd_d[f0:f1, :].rearrange("(c p) n -> p c n", p=128), writes=[pre + 'stg'])
                for c in range(gc):
                    S.op('dve', lambda e, c=c, sl=sl: e.tensor_tensor(out=wd_sl[sl][:, c, :], in0=stg[:, c, :], in1=gp1[:], op=ALU.mult),
                         reads=[pre + 'stg', 'gp1'], writes=[pre + 'wd%d' % sl])

            def upgate(sl, gc, tb, a_sl):
                hkeys = [('hT', 4 * tb + i) for i in range(4)]
                for c in range(gc):
                    gb = c % 2
                    for k in range(KC):
                        S.op('pe', lambda e, k=k, c=c, gb=gb: e.matmul(ps[:, gb, :], lhsT=wg_sl[sl][:, k, 128 * c:128 * c + 128], rhs=hT[:, k, 512 * tb:512 * tb + 512],
                                                                     start=(k == 0), stop=(k == KC - 1)),
                             reads=hkeys + [pre + 'wg%d' % sl], writes=[PS(gb)], sig=(k == KC - 1))
                    for k in range(KC):
                        S.op('pe', lambda e, k=k, c=c, gb=gb: e.matmul(ps[:, 2 + gb, :], lhsT=wu_sl[sl][:, k, 128 * c:128 * c + 128], rhs=hT[:, k, 512 * tb:512 * tb + 512],
                                                                     start=(k == 0), stop=(k == KC - 1)),
                             reads=hkeys + [pre + 'wu%d' % sl], writes=[PS(2 + gb)], sig=(k == KC - 1))
                    S.op('act', lambda e, gb=gb: e.activation(out=sg[gb], in_=ps[:, gb, :], func=AF.Silu), reads=[PS(gb)], writes=[pre + 'sg%d' % gb])
                    S.op('dve', lambda e, c=c, gb=gb: e.tensor_tensor(out=aT[a_sl][:, c, :], in0=sg[gb], in1=ps[:, 2 + gb, :], op=ALU.mult),
                         reads=[pre + 'sg%d' % gb, PS(2 + gb)], writes=[pre + 'aT%d' % a_sl])

            def down(sl, gc, ei, tb, a_sl, last_group, first_group):
                for t4 in range(4):
                    j = 4 * tb + t4
                    yb = 4 + 2 * (t4 % 2)
                    for n in range(2):
                        for c in range(gc):
                            S.op('pe', lambda e, n=n, c=c, t4=t4, yb=yb: e.matmul(ps[:, yb + n, :], lhsT=aT[a_sl][:, c, 128 * t4:128 * t4 + 128],
                                                                                rhs=wd_sl[sl][:, c, 512 * n:512 * n + 512],
                                                                                start=(c == 0), stop=(c == gc - 1)),
                                 reads=[pre + 'aT%d' % a_sl, pre + 'wd%d' % sl], writes=[PS(yb + n)], sig=(c == gc - 1))
                    if moe:
                        S.op('dve', lambda e, j=j, yb=yb: e.scalar_tensor_tensor(out=X[:, j, :], in0=ps2(yb), scalar=gate[:, j, ei:ei + 1], in1=X[:, j, :],
                                                                               op0=ALU.mult, op1=ALU.add),
                             reads=[PS(yb), PS(yb + 1), ('X', j), ('gate', j)], writes=[('X', j)])
                    elif first_group:
                        S.op('dve', lambda e, j=j, yb=yb: e.scalar_tensor_tensor(out=X[:, j, :], in0=X[:, j, :], scalar=ALPHA, in1=ps2(yb),
                                                                               op0=ALU.mult, op1=ALU.add),
                             reads=[PS(yb), PS(yb + 1), ('X', j)], writes=[('X', j)])
                    else:
                        S.op('dve', lambda e, j=j, yb=yb: e.tensor_tensor(out=X[:, j, :], in0=ps2(yb), in1=X[:, j, :], op=ALU.add),
                             reads=[PS(yb), PS(yb + 1), ('X', j)], writes=[('X', j)])
                    if last_group:
                        emit_ln(j)

            if DBG_SKIPCOMPUTE:
                return
            G = len(groups)
            items = [(gi, tb) for gi in range(G) for tb in range(4)]
            load_group(0)
            if G > 1:
                load_group(1)
            upgate(0, groups[0][2], 0, 0)
            for n, (gi, tb) in enumerate(items):
                if n + 1 < len(items):
                    gi2, tb2 = items[n + 1]
                    upgate(gi2 % 2, groups[gi2][2], tb2, (n + 1) % 2)
                down(gi % 2, groups[gi][2], groups[gi][0], tb, n % 2, gi == G - 1, gi == 0)
                if tb == 3 and gi + 2 < G:
                    load_group(gi + 2)

        def stage3():
            AR.reset()
            winu = AR.alloc('b_winu', [KC, D], BF16)
            winv = AR.alloc('b_winv', [KC, D], BF16)
            wout = AR.alloc('b_wout', [KC, D], BF16)
            stg = AR.alloc('b_stg', [D], F32)
            wsT = AR.alloc('b_wsT', [8, 128], BF16)
            bsT = AR.alloc('b_bsT', [8], F32)
            sgbc = AR.alloc('b_sgbc', [2, D], F32)
            u32 = [AR.alloc('b_u32%d' % i, [D], F32) for i in range(2)]
            v32 = [AR.alloc('b_v32%d' % i, [D], F32) for i in range(2)]
            vln = [AR.alloc('b_vln%d' % i, [D], BF16) for i in range(2)]
            gat = [AR.alloc('b_gat%d' % i, [D], BF16) for i in range(2)]
            gatT = [AR.alloc('b_gatT%d' % i, [8, 128], BF16) for i in range(2)]
            sm = [AR.alloc('b_sm%d' % i, [16], F32) for i in range(2)]
            wv = w_view(od_w_in_d)
            S.dma('pool', winu, wv[:, :, 0:D], writes=['b_winu'])
            S.dma('pool', winv, wv[:, :, D:2 * D], writes=['b_winv'])
            S.dma('pool', wsT, wsT_d, writes=['b_wsT'])
            S.dma('sp', bsT, bsT_d, writes=['b_bsT'])
            S.dma('sp', sgbc[:, 0, :], sg_g_d.partition_broadcast(128).rearrange("p o d -> p (o d)"), writes=['b_sgbc'], sem='b_sgbc')
            S.dma('sp', sgbc[:, 1, :], sg_b_d.partition_broadcast(128).rearrange("p o d -> p (o d)"), writes=['b_sgbc'], sem='b_sgbc')
            load_gp1(1, 0)
            load_ln_params(1, 0)
            wo_v = w_view(od_w_out_d)
            for c in range(KC):
                S.dma('sp', stg, wo_v[:, c, :], writes=['b_stg'])
                S.op('dve', lambda e, c=c: e.tensor_tensor(out=wout[:, c, :], in0=stg, in1=gp1[:], op=ALU.mult),
                     reads=['b_stg', 'gp1'], writes=['b_wout'])
            make_hT_all(1, 0)

            def names3(j):
                b = j % 2
                return dict(b=b, sw=4 * b + 2, u_=u32[b], v_=v32[b], vl_=vln[b], ga_=gat[b], gT_=gatT[b], sm_=sm[b],
                            ku='b_u32%d' % b, kv='b_v32%d' % b, kvl='b_vln%d' % b, kga='b_gat%d' % b, kgT='b_gatT%d' % b, ksm='b_sm%d' % b)

            def st3A(j):
                n_ = names3(j)
                b = n_['b']
                hk = ('hT', j)
                for (w_, dst, dkey, wkey, b0) in ((winu, n_['u_'], n_['ku'], 'b_winu', 4 * b), (winv, n_['v_'], n_['kv'], 'b_winv', 4 * b)):
                    for n in range(2):
                        for k in range(KC):
                            S.op('pe', lambda e, n=n, k=k, w_=w_, b0=b0: e.matmul(ps[:, b0 + n, :], lhsT=hT[:, k, 128 * j:128 * j + 128], rhs=w_[:, k, 512 * n:512 * n + 512],
                                                                                start=(k == 0), stop=(k == KC - 1)),
                                 reads=[hk, wkey], writes=[PS(b0 + n)], sig=(k == KC - 1))
                    S.op('act', lambda e, dst=dst, b0=b0: e.activation(out=dst, in_=ps2(b0), func=AF.Gelu), reads=[PS(b0), PS(b0 + 1)], writes=[dkey])

            def st3B(j):
                n_ = names3(j)
                v_, vl_, sm_, kv, kvl, ksm = n_['v_'], n_['vl_'], n_['sm_'], n_['kv'], n_['kvl'], n_['ksm']
                st6 = sm_[:, 0:12].rearrange("p (a b) -> p a b", a=2)
                mv, rstd, nmr = sm_[:, 12:14], sm_[:, 14:15], sm_[:, 15:16]
                for hlf in range(2):
                    S.op('dve', lambda e, hlf=hlf: e.bn_stats(out=st6[:, hlf, :], in_=v_[:, 512 * hlf:512 * hlf + 512]), reads=[kv], writes=[ksm])
                S.op('dve', lambda e: e.bn_aggr(out=mv, in_=sm_[:, 0:12]), reads=[ksm], writes=[ksm])
                S.op('act', lambda e: e.activation(out=rstd, in_=mv[:, 1:2], func=AF.Ln, bias=epsb[:, 0:1], scale=1.0), reads=[ksm, 'epsb'], writes=[ksm])
                S.op('act', lambda e: e.activation(out=rstd, in_=rstd, func=AF.Exp, scale=-0.5), reads=[ksm], writes=[ksm])
                S.op('dve', lambda e: e.tensor_scalar(out=nmr, in0=mv[:, 0:1], scalar1=rstd, scalar2=-1.0, op0=ALU.mult, op1=ALU.mult), reads=[ksm], writes=[ksm])
                S.op('act', lambda e: e.activation(out=v_, in_=v_, func=AF.Identity, bias=nmr, scale=rstd), reads=[kv, ksm], writes=[kv])
                S.op('dve', lambda e: e.tensor_tensor(out=v_, in0=v_, in1=sgbc[:, 0, :], op=ALU.mult), reads=[kv, 'b_sgbc'], writes=[kv])
                S.op('dve', lambda e: e.tensor_tensor(out=vl_, in0=v_, in1=sgbc[:, 1, :], op=ALU.add), reads=[kv, 'b_sgbc'], writes=[kvl])

            def st3C(j):
                n_ = names3(j)
                sw, u_, v_, vl_, ga_, ku, kv, kvl, kga = n_['sw'], n_['u_'], n_['v_'], n_['vl_'], n_['ga_'], n_['ku'], n_['kv'], n_['kvl'], n_['kga']
                for g in range(8):
                    S.op('pe', lambda e, g=g: e.matmul(ps[:, sw + g // 4, 128 * (g % 4):128 * (g % 4) + 128], lhsT=wsT[:, g, :], rhs=vl_[:, 128 * g:128 * g + 128],
                                                       start=True, stop=True),
                         reads=['b_wsT', kvl], writes=[PS(sw + g // 4)], sig=(g % 4 == 3))
                S.op('dve', lambda e: e.tensor_tensor(out=v_.rearrange("p (a b) -> p a b", a=8), in0=ps2(sw).rearrange("p (a b) -> p a b", a=8),
                                                      in1=bsT[:, 0:8].unsqueeze(2).to_broadcast([128, 8, 128]), op=ALU.add),
                     reads=[PS(sw), PS(sw + 1), 'b_bsT'], writes=[kv])
                S.op('dve', lambda e: e.tensor_tensor(out=ga_, in0=v_, in1=u_, op=ALU.mult), reads=[kv, ku], writes=[kga])

            def st3D(j):
                n_ = names3(j)
                b, ga_, gT_, kga, kgT = n_['b'], n_['ga_'], n_['gT_'], n_['kga'], n_['kgT']
                pbank = 4 * b
                psb = ps[:, pbank, :].bitcast(BF16)
                for g in range(8):
                    S.op('pe', lambda e, g=g: e.transpose(psb[:, 128 * g:128 * g + 128], ga_[:, 128 * g:128 * g + 128], identb[:]),
                         reads=[kga, 'identb'], writes=[PS(pbank)], sig=(g == 7))
                S.op('act', lambda e: e.activation(out=gT_, in_=psb.rearrange("p (a b) -> p a b", a=8), func=AF.Identity), reads=[PS(pbank)], writes=[kgT])

            def st3E(j):
                n_ = names3(j)
                sw, gT_, kgT = n_['sw'], n_['gT_'], n_['kgT']
                for n in range(2):
                    for c in range(KC):
                        S.op('pe', lambda e, n=n, c=c: e.matmul(ps[:, sw + n, :], lhsT=gT_[:, c, :], rhs=wout[:, c, 512 * n:512 * n + 512],
                                                               start=(c == 0), stop=(c == KC - 1)),
                             reads=[kgT, 'b_wout'], writes=[PS(sw + n)], sig=(c == KC - 1))
                S.op('dve', lambda e: e.scalar_tensor_tensor(out=X[:, j, :], in0=X[:, j, :], scalar=ALPHA, in1=ps2(sw), op0=ALU.mult, op1=ALU.add),
                     reads=[('X', j), PS(sw), PS(sw + 1)], writes=[('X', j)])
                emit_ln(j)

            S.pipeline([st3A, st3B, st3C, st3D, st3E], NTILE, order=[3, 4, 2, 0, 1])

        for s in stages:
            if s == 1:
                stage1()
            elif s == 2:
                stage_ffn(0, False)
            elif s == 3:
                stage3()
            elif s == 4:
                stage_ffn(1, True)

        for g4 in range(4):
            S.dma('sp', out_d[512 * g4:512 * g4 + 512, :].rearrange("(j p) d -> p j d", p=128), X[:, 4 * g4:4 * g4 + 4, :],
                  reads=[('X', 4 * g4 + i) for i in range(4)], sem='out')
        S.wait_sem_final('sp', 'out')
        S.emit()
    return nc


def _const_tables(q):
    t0 = q * NT
    band = np.zeros((128, 7, 4, 128), np.float32)
    rc = np.zeros((2, 4, 128), np.float32)
    s_loc = np.arange(128)[:, None]
    t_loc = np.arange(128)[None, :]
    for g, w in enumerate(POOL_WINDOWS):
        r = w // 2

        def mats(tile_base):
            tg = tile_base + t_loc
            cnt = (np.minimum(tg + r + 1, SEQ) - np.maximum(tg - r, 0)).astype(np.float32)
            outs = []
            for off in (-128, 0, 128):
                sg_ = tile_base + off + s_loc
                m = ((np.abs(sg_ - tg) <= r) & (sg_ >= 0) & (sg_ < SEQ)).astype(np.float32)
                if off == 0:
                    m = m - np.eye(128, dtype=np.float32) * cnt
                outs.append(m)
            return outs, cnt
        mid, _ = mats(t0 + 128 * 5)
        first, cf = mats(t0)
        last, cl = mats(t0 + NT - 128)
        band[:, 0, g], band[:, 1, g], band[:, 2, g] = mid
        band[:, 3, g], band[:, 4, g] = first[0], first[1]
        band[:, 5, g], band[:, 6, g] = last[1], last[2]
        rc[0, g] = 1.0 / cf[0]
        rc[1, g] = 1.0 / cl[0]
    i = np.arange(128)[:, None]
    jj = np.arange(384)[None, :]
    dist = np.abs(jj - 128 - i).astype(np.float32)
    dmask = np.zeros((128, 3, 384), np.float32)
    for var, base in enumerate((t0, t0 + 128 * 5, t0 + NT - 128)):
        kp = base - 128 + jj
        ok = (dist <= 128) & (kp >= 0) & (kp < SEQ)
        dmask[:, var, :] = np.where(ok, 0.0, -1e30)
    return band, rc.reshape(1, -1), dist, dmask


_PROG_CACHE = {}


def _get_prog(stages):
    if stages not in _PROG_CACHE:
        _PROG_CACHE[stages] = build_program(stages)
    return _PROG_CACHE[stages]


def _in_maps(inputs, x_full, stages=(1, 2, 3, 4)):
    f = lambda a: np.ascontiguousarray(np.asarray(a, dtype=np.float32))
    shared = dict(
        ident=np.eye(128, dtype=np.float32),
        ada_w=f(inputs["ada_w"]), ada_b=f(inputs["ada_b"]), ln_g=f(inputs["ln_g"]), ln_b=f(inputs["ln_b"]),
        ev_w_in=f(inputs["ev_w_in"][0]), ev_pool_w=f(inputs["ev_pool_w"][0]),
        pscaleT=f(np.asarray(inputs["ev_pool_scale"][0]).reshape(4, 128).T),
        ev_sink=f(np.asarray(inputs["ev_sink"][0]).reshape(1, 8)), ev_w_out=f(inputs["ev_w_out"][0]),
        od_w_in=f(inputs["od_w_in"][0]), od_sg_ln_g=f(np.asarray(inputs["od_sg_ln_g"][0]).reshape(1, D)),
        od_sg_ln_b=f(np.asarray(inputs["od_sg_ln_b"][0]).reshape(1, D)),
        wsT=f(np.asarray(inputs["od_w_s"][0]).transpose(2, 0, 1)),
        bsT=f(np.asarray(inputs["od_b_s"][0]).T),
        od_w_out=f(inputs["od_w_out"][0]),
        ffn_w_gate=f(inputs["ffn_w_gate"][0]), ffn_w_up=f(inputs["ffn_w_up"][0]), ffn_w_down=f(inputs["ffn_w_down"][0]),
    )
    if 4 in stages:
        shared.update(moe_w_router=f(inputs["moe_w_router"][0]), moe_w_gate=f(inputs["moe_w_gate"][0][:NEXP_DECL]),
                      moe_w_up=f(inputs["moe_w_up"][0][:NEXP_DECL]), moe_w_down=f(inputs["moe_w_down"][0][:NEXP_DECL]))
    c = np.asarray(inputs["c"], dtype=np.float32)
    maps = []
    for core in range(NCORES):
        b, q = core // 4, core % 4
        t0 = q * NT
        xh = np.zeros((256, D), np.float32)
        if q > 0:
            xh[0:128] = x_full[b, t0 - 128:t0]
        if q < 3:
            xh[128:256] = x_full[b, t0 + NT:t0 + NT + 128]
        band, rc, dist, dmask = _const_tables(q)
        m = dict(shared)
        m.update(x=np.ascontiguousarray(x_full[b, t0:t0 + NT]), xh=xh,
                 cT=np.ascontiguousarray(c[b].reshape(KC, 128).T),
                 band=band, rc=rc, dist=dist, dmask=dmask)
        maps.append(m)
    return maps


def run_stages(inputs, x_full, stages):
    nc = _get_prog(tuple(stages))
    maps = _in_maps(inputs, x_full, tuple(stages))
    res = run_bass_kernel_spmd(nc, maps, core_ids=list(range(NCORES)))
    out = np.empty((2, SEQ, D), np.float32)
    for core in range(NCORES):
        b, q = core // 4, core % 4
        out[b, q * NT:(q + 1) * NT] = res.results[core]["out"]
    return out


def kernel(**inputs):
    x = np.asarray(inputs["x"], dtype=np.float32)
    return run_stages(inputs, x, (1, 2, 3, 4))
```
